# Optimizing a Trainium2 kernel written in Bass

```python
import math
import jax, jax.numpy as jnp
from jax import lax
import numpy as np

D_MODEL = 1024
BATCH = 4
SEQ = 4096
DEPTH = 2
DEC_BATCH = 128
DEC_SEQ = 1
PAST_LEN = 2048
PAGE_SIZE = 128

HEAD_DIM = 64
N_HEADS_A = (D_MODEL // 2) // HEAD_DIM
D_A = N_HEADS_A * HEAD_DIM
D_B = D_MODEL // 4
D_C = D_MODEL // 4
D_MIX = D_A + D_B + D_C
D_IN = 3 * D_A + 2 * D_B + D_C
DILATED = ((128, 1), (512, 4), (2048, 16))
WINDOW_MAX = 2048
BLK = 128
CONV_WIDTH = 31
POOL_WINDOWS = (2, 4, 8, 16)
POOL_GROUP = D_C // len(POOL_WINDOWS)
POOL_PREFIX = max(POOL_WINDOWS) - 1
NUM_BUCKETS = 32
MAX_EXACT = NUM_BUCKETS // 2
REL_MAX_DIST = WINDOW_MAX
D_FF = ((8 * D_MODEL // 3 + 255) // 256) * 256
EPS = 1e-6
NEG_INF = -1e30

kernel_name = 'hybrid_dilated_conv_pool_decoder_step'


def rmsnorm(x, g):
    xf = x.astype(jnp.float32)
    y = xf * lax.rsqrt(jnp.mean(xf * xf, axis=-1, keepdims=True) + EPS)
    return (y * g.astype(jnp.float32)).astype(x.dtype)


def layernorm(x, g, b):
    xf = x.astype(jnp.float32)
    mu = jnp.mean(xf, axis=-1, keepdims=True)
    xc = xf - mu
    y = xc * lax.rsqrt(jnp.mean(xc * xc, axis=-1, keepdims=True) + EPS)
    return (y * g.astype(jnp.float32) + b.astype(jnp.float32)).astype(x.dtype)


def swiglu(x, w_gu, w_down):
    g, u = jnp.split(x @ w_gu, 2, axis=-1)
    return (jax.nn.silu(g) * u) @ w_down


def rel_bucket(dist):
    small = dist < MAX_EXACT
    large = MAX_EXACT + (jnp.log(jnp.maximum(dist, 1).astype(jnp.float32) / MAX_EXACT)
                         / math.log(REL_MAX_DIST / MAX_EXACT) * (NUM_BUCKETS - MAX_EXACT)).astype(jnp.int32)
    return jnp.where(small, dist, jnp.minimum(large, NUM_BUCKETS - 1))


def branch_prompt(q, k, v, rel_bias, window, dil):
    bsz, seq, nh, hd = q.shape
    span = window // dil
    L = seq // dil
    nb = -(-L // BLK)
    Lp = nb * BLK

    def to_res(a):
        return a.reshape(bsz, L, dil, nh, hd).transpose(0, 2, 1, 3, 4).reshape(bsz * dil, L, nh, hd)

    def key_blocks(a):
        a = jnp.pad(to_res(a), ((0, 0), (BLK, Lp - L), (0, 0), (0, 0))).reshape(bsz * dil, nb + 1, BLK, nh, hd)
        return jnp.concatenate([a[:, :-1], a[:, 1:]], axis=2)

    qb = jnp.pad(to_res(q), ((0, 0), (0, Lp - L), (0, 0), (0, 0))).reshape(bsz * dil, nb, BLK, nh, hd)
    kb = key_blocks(k)
    vb = key_blocks(v)
    logits = jnp.einsum('nbqhd,nbkhd->nbhqk', qb, kb, preferred_element_type=jnp.float32)
    qi = jnp.arange(BLK)[:, None]
    kk = jnp.arange(2 * BLK)[None, :]
    dist = BLK + qi - kk
    key_idx = jnp.arange(nb)[:, None] * BLK - BLK + kk
    mask = ((dist >= 0) & (dist <= span))[None] & (key_idx >= 0)[:, None, :]
    bias = rel_bias[rel_bucket(dil * jnp.clip(dist, 0, span))]
    logits = logits + jnp.transpose(bias, (2, 0, 1)).astype(jnp.float32)
    logits = jnp.where(mask[None, :, None], logits, NEG_INF)
    m = jnp.max(logits, axis=-1)
    p = jnp.exp(logits - m[..., None])
    s = jnp.sum(p, axis=-1)
    num = jnp.einsum('nbhqk,nbkhd->nbqhd', p, vb.astype(jnp.float32))

    def from_res(a):
        tail = a.shape[3:]
        a = a.reshape((bsz, dil, Lp) + tail)[:, :, :L]
        return jnp.moveaxis(a, 1, 2).reshape((bsz, seq) + tail)

    return (from_res(jnp.swapaxes(m, 2, 3)), from_res(jnp.swapaxes(s, 2, 3)), from_res(num))


def branch_sample(q, k_buf, v_buf, k_new, v_new, rel_bias, window, dil):
    W = k_buf.shape[1]
    T = q.shape[1]
    span = window // dil
    j = jnp.arange(span + 1)
    idx = W + jnp.arange(T)[:, None] - dil * j[None, :]
    valid = idx >= 0
    from_new = idx >= W

    def gather(buf, new):
        gb = buf[:, jnp.clip(idx, 0, W - 1)]
        gn = new[:, jnp.clip(idx - W, 0, T - 1)]
        return jnp.where(from_new[None, :, :, None, None], gn, gb)

    kg = gather(k_buf, k_new)
    vg = gather(v_buf, v_new)
    logits = jnp.einsum('bthd,btjhd->bthj', q, kg, preferred_element_type=jnp.float32)
    bias = rel_bias[rel_bucket(dil * j)]
    logits = logits + bias.T.astype(jnp.float32)
    logits = jnp.where(valid[None, :, None, :], logits, NEG_INF)
    m = jnp.max(logits, axis=-1)
    p = jnp.exp(logits - m[..., None])
    s = jnp.sum(p, axis=-1)
    num = jnp.einsum('bthj,btjhd->bthd', p, vg.astype(jnp.float32))
    return m, s, num


def merge_branches(results):
    big = jnp.max(jnp.stack([r[0] for r in results]), axis=0)
    num = 0.0
    den = 0.0
    for m, s, n in results:
        w = jnp.exp(m - big)
        num = num + w[..., None] * n
        den = den + w * s
    return num / den[..., None]


def conv_module(glu_ext, w_dw, b_dw, ln_g, ln_b, w_pw):
    y = lax.conv_general_dilated(glu_ext, w_dw[:, None, :], window_strides=(1,), padding='VALID',
                                 dimension_numbers=('NWC', 'WIO', 'NWC'), feature_group_count=D_B) + b_dw
    y = layernorm(y, ln_g, ln_b)
    return jax.nn.silu(y) @ w_pw


def pool_mixer(c_ext, pos0, w_pool, scale):
    bsz, n_ext, _ = c_ext.shape
    P = POOL_PREFIX
    T = n_ext - P
    cf = c_ext.astype(jnp.float32)
    cs = jnp.concatenate([jnp.zeros((bsz, 1, D_C), jnp.float32), jnp.cumsum(cf, axis=1)], axis=1)
    end = cs[:, P + 1:]
    pos = pos0 + jnp.arange(T)
    outs = []
    for g, w in enumerate(POOL_WINDOWS):
        sl = slice(g * POOL_GROUP, (g + 1) * POOL_GROUP)
        start = cs[:, P + 1 - w:P + 1 - w + T, sl]
        cnt = jnp.minimum(w, pos + 1).astype(jnp.float32)[None, :, None]
        outs.append((end[..., sl] - start) / cnt - cf[:, P:, sl])
    d = jnp.stack(outs, axis=2)
    y = jnp.einsum('btgc,gce->btge', d, w_pool.astype(jnp.float32)).reshape(bsz, T, D_C)
    return (y * scale.astype(jnp.float32)).astype(c_ext.dtype)


def layer_pre(h, g_f1, w_f1_gu, w_f1_down, g_mix, w_in, g_q, g_k):
    h = h + 0.5 * swiglu(rmsnorm(h, g_f1), w_f1_gu, w_f1_down)
    u = rmsnorm(h, g_mix) @ w_in
    bsz, T, _ = u.shape
    q, k, v, b_val, b_gate, c = jnp.split(
        u, [D_A, 2 * D_A, 3 * D_A, 3 * D_A + D_B, 3 * D_A + 2 * D_B], axis=-1)
    q = rmsnorm(q.reshape(bsz, T, N_HEADS_A, HEAD_DIM), g_q) * (HEAD_DIM ** -0.5)
    k = rmsnorm(k.reshape(bsz, T, N_HEADS_A, HEAD_DIM), g_k)
    v = v.reshape(bsz, T, N_HEADS_A, HEAD_DIM)
    glu = b_val * jax.nn.sigmoid(b_gate)
    return h, q, k, v, glu, c


def layer_post(h, attn, conv_out, pool_out, w_out, g_f2, w_f2_gu, w_f2_down):
    bsz, T, _ = h.shape
    mix = jnp.concatenate([attn.reshape(bsz, T, D_A).astype(h.dtype), conv_out, pool_out], axis=-1) @ w_out
    h = h + mix
    return h + 0.5 * swiglu(rmsnorm(h, g_f2), w_f2_gu, w_f2_down)


def setup_inputs(seed: int = 0) -> dict:
    key = jax.random.key(seed)
    ks = jax.random.split(key, 32)
    f32 = jnp.float32

    def nrm(k, shape, scale):
        return jax.random.normal(k, shape, f32) * scale

    w_buf = min(WINDOW_MAX, PAST_LEN)
    return {
        'x_prompt': nrm(ks[0], (BATCH, SEQ, D_MODEL), 1.0),
        'x_sample': nrm(ks[1], (DEC_BATCH, DEC_SEQ, D_MODEL), 1.0),
        'cache_attn_k': nrm(ks[2], (DEPTH, DEC_BATCH, w_buf, N_HEADS_A, HEAD_DIM), 1.0),
        'cache_attn_v': nrm(ks[3], (DEPTH, DEC_BATCH, w_buf, N_HEADS_A, HEAD_DIM), 1.0),
        'cache_conv': nrm(ks[4], (DEPTH, DEC_BATCH, CONV_WIDTH - 1, D_B), 0.5),
        'cache_pool': nrm(ks[5], (DEPTH, DEC_BATCH, POOL_PREFIX, D_C), 1.0),
        'rel_bias': nrm(ks[6], (NUM_BUCKETS, N_HEADS_A), 0.5),
        'g_ffn1': 1.0 + nrm(ks[7], (DEPTH, D_MODEL), 0.05),
        'w_ffn1_gu': nrm(ks[8], (DEPTH, D_MODEL, 2 * D_FF), D_MODEL ** -0.5),
        'w_ffn1_down': nrm(ks[9], (DEPTH, D_FF, D_MODEL), D_FF ** -0.5),
        'g_mix': 1.0 + nrm(ks[10], (DEPTH, D_MODEL), 0.05),
        'w_in': nrm(ks[11], (DEPTH, D_MODEL, D_IN), D_MODEL ** -0.5),
        'g_q': 1.0 + nrm(ks[12], (DEPTH, HEAD_DIM), 0.05),
        'g_k': 1.0 + nrm(ks[13], (DEPTH, HEAD_DIM), 0.05),
        'conv_w': nrm(ks[14], (DEPTH, CONV_WIDTH, D_B), CONV_WIDTH ** -0.5),
        'conv_b': nrm(ks[15], (DEPTH, D_B), 0.02),
        'conv_ln_g': 1.0 + nrm(ks[16], (DEPTH, D_B), 0.05),
        'conv_ln_b': nrm(ks[17], (DEPTH, D_B), 0.02),
        'conv_pw': nrm(ks[18], (DEPTH, D_B, D_B), D_B ** -0.5),
        'pool_w': nrm(ks[19], (DEPTH, len(POOL_WINDOWS), POOL_GROUP, POOL_GROUP), POOL_GROUP ** -0.5),
        'pool_scale': 1.0 + nrm(ks[20], (DEPTH, D_C), 0.1),
        'w_out': nrm(ks[21], (DEPTH, D_MIX, D_MODEL), D_MIX ** -0.5),
        'g_ffn2': 1.0 + nrm(ks[22], (DEPTH, D_MODEL), 0.05),
        'w_ffn2_gu': nrm(ks[23], (DEPTH, D_MODEL, 2 * D_FF), D_MODEL ** -0.5),
        'w_ffn2_down': nrm(ks[24], (DEPTH, D_FF, D_MODEL), D_FF ** -0.5),
    }


def reference(x_prompt, x_sample, cache_attn_k, cache_attn_v, cache_conv, cache_pool, rel_bias,
              g_ffn1, w_ffn1_gu, w_ffn1_down, g_mix, w_in, g_q, g_k, conv_w, conv_b, conv_ln_g,
              conv_ln_b, conv_pw, pool_w, pool_scale, w_out, g_ffn2, w_ffn2_gu, w_ffn2_down):
    hp = x_prompt
    hs = x_sample
    pk, pv, pc, pp = [], [], [], []
    sk, sv, sc, sp = [], [], [], []
    for l in range(DEPTH):
        pre = (g_ffn1[l], w_ffn1_gu[l], w_ffn1_down[l], g_mix[l], w_in[l], g_q[l], g_k[l])
        conv_p = (conv_w[l], conv_b[l], conv_ln_g[l], conv_ln_b[l], conv_pw[l])
        post = (w_out[l], g_ffn2[l], w_ffn2_gu[l], w_ffn2_down[l])

        hp, q, k, v, glu, c = layer_pre(hp, *pre)
        attn = merge_branches([branch_prompt(q, k, v, rel_bias, w, d) for (w, d) in DILATED])
        bsz = hp.shape[0]
        glu_ext = jnp.concatenate([jnp.zeros((bsz, CONV_WIDTH - 1, D_B), glu.dtype), glu], axis=1)
        c_ext = jnp.concatenate([jnp.zeros((bsz, POOL_PREFIX, D_C), c.dtype), c], axis=1)
        conv_out = conv_module(glu_ext, *conv_p)
        pool_out = pool_mixer(c_ext, 0, pool_w[l], pool_scale[l])
        hp = layer_post(hp, attn, conv_out, pool_out, *post)
        keep = min(WINDOW_MAX, k.shape[1])
        pk.append(k[:, -keep:])
        pv.append(v[:, -keep:])
        pc.append(glu_ext[:, -(CONV_WIDTH - 1):])
        pp.append(c_ext[:, -POOL_PREFIX:])

        hs, q, k, v, glu, c = layer_pre(hs, *pre)
        k_buf = cache_attn_k[l]
        v_buf = cache_attn_v[l]
        attn = merge_branches([branch_sample(q, k_buf, v_buf, k, v, rel_bias, w, d) for (w, d) in DILATED])
        glu_ext = jnp.concatenate([cache_conv[l], glu], axis=1)
        c_ext = jnp.concatenate([cache_pool[l], c], axis=1)
        conv_out = conv_module(glu_ext, *conv_p)
        pool_out = pool_mixer(c_ext, PAST_LEN, pool_w[l], pool_scale[l])
        hs = layer_post(hs, attn, conv_out, pool_out, *post)
        sk.append(k)
        sv.append(v)
        sc.append(glu_ext[:, -(CONV_WIDTH - 1):])
        sp.append(c_ext[:, -POOL_PREFIX:])

    return (hp, hs, jnp.stack(pk), jnp.stack(pv), jnp.stack(pc), jnp.stack(pp),
            jnp.stack(sk), jnp.stack(sv), jnp.stack(sc), jnp.stack(sp))
```

```python
import math
import numpy as np
import concourse.bass as bass
import concourse.mybir as mybir
from concourse.bass_utils import run_bass_kernel_spmd

F32, BF16 = mybir.dt.float32, mybir.dt.bfloat16
AF = mybir.ActivationFunctionType
ALU = mybir.AluOpType
AX = mybir.AxisListType

NTOK, NS = 2048, 16
NCOL = NTOK + NS
TILES = [(0, 512), (512, 512), (1024, 512), (1536, 512), (2048, 16)]
SUPER = [[0, 1], [2, 3, 4]]
SUP0 = [0, 1024]
SW = 1040
EPS = 1e-6
DILS = (1, 4, 16)
DFF = 2816
NCORES = 8
NEG = -30000.0


def _bucket(dist):
    if dist < 16:
        return dist
    v = 16 + int(np.float32(np.log(np.float32(max(dist, 1)) / np.float32(16)) / np.float32(math.log(2048 / 16)) * np.float32(16)))
    return min(v, 31)


def _bucket_np(dist):
    dist = np.asarray(dist)
    large = 16 + (np.log(np.maximum(dist, 1).astype(np.float32) / 16) / math.log(2048 / 16) * 16).astype(np.int32)
    return np.where(dist < 16, dist, np.minimum(large, 31))


class Rec:
    __slots__ = ("eng", "fn", "deps", "inc", "val", "kind", "sem", "dval")


class Prog:
    ENG = ("pe", "act", "dve", "pool", "sp")

    def __init__(self):
        self.recs = {e: [] for e in self.ENG}
        self.lastw = {}
        self.readers = {}
        self.fence_rec = None
        self.since = []

    def fence(self, fn):
        r = Rec()
        r.eng, r.fn, r.kind, r.inc, r.val, r.sem, r.dval = "dve", fn, "c", False, 0, None, 0
        deps = {}
        for e in self.ENG:
            for q in reversed(self.recs[e]):
                if q.kind == "c":
                    deps[id(q)] = q
                    break
        for q in self.since:
            if q.kind != "c":
                deps[id(q)] = q
        if self.fence_rec is not None:
            deps[id(self.fence_rec)] = self.fence_rec
        r.deps = list(deps.values())
        self.recs["dve"].append(r)
        self.fence_rec = r
        self.since = []
        self.lastw = {}
        self.readers = {}
        return r

    def op(self, eng, fn, reads=(), writes=(), kind="c"):
        r = Rec()
        r.eng, r.fn, r.kind, r.inc, r.val, r.sem, r.dval = eng, fn, kind, False, 0, None, 0
        deps = {}
        if self.fence_rec is not None:
            deps[id(self.fence_rec)] = self.fence_rec
        self.since.append(r)
        for k in reads:
            w = self.lastw.get(k)
            if w is not None:
                deps[id(w)] = w
        for k in writes:
            w = self.lastw.get(k)
            if w is not None:
                deps[id(w)] = w
            for q in self.readers.get(k, ()):
                deps[id(q)] = q
        for k in reads:
            self.readers.setdefault(k, []).append(r)
        for k in writes:
            self.lastw[k] = r
            self.readers[k] = []
        deps.pop(id(r), None)
        r.deps = list(deps.values())
        self.recs[eng].append(r)
        return r

    def dma(self, eng, fn, reads=(), writes=()):
        return self.op(eng, fn, reads, writes, kind="d")


def mm(out, lhsT, rhs, start=True, stop=True):
    return lambda e: e.matmul(out, lhsT, rhs, start=start, stop=stop)


def tr(out, in_, ident):
    return lambda e: e.transpose(out, in_, ident)


def actf(out, in_, func, bias=None, scale=None):
    kw = {}
    if bias is not None:
        kw["bias"] = bias
    if scale is not None:
        kw["scale"] = scale
    return lambda e: e.activation(out=out, in_=in_, func=func, **kw)


def tsc(out, in0, s1, s2, op0, op1=None):
    if op1 is None:
        return lambda e: e.tensor_scalar(out=out, in0=in0, scalar1=s1, scalar2=None, op0=op0)
    return lambda e: e.tensor_scalar(out=out, in0=in0, scalar1=s1, scalar2=s2, op0=op0, op1=op1)


def stt(out, in0, scalar, in1, op0, op1):
    return lambda e: e.scalar_tensor_tensor(out=out, in0=in0, scalar=scalar, in1=in1, op0=op0, op1=op1)


def tt(out, in0, in1, op):
    return lambda e: e.tensor_tensor(out=out, in0=in0, in1=in1, op=op)


def rcp(out, in_):
    return lambda e: e.reciprocal(out=out, in_=in_)


def cp(out, in_):
    return lambda e: e.tensor_copy(out=out, in_=in_)


def dm(out, in_):
    return lambda e: e.dma_start(out=out, in_=in_)


def build_program():
    import os
    KSTOP = int(os.environ.get("KSTOP", "99"))
    KSUB = int(os.environ.get("KSUB", "99"))
    KSMALL = int(os.environ.get("KSMALL", "0"))
    nc = bass.Bass("TRN2", target_bir_lowering=False)
    P = Prog()

    def din(name, shape, dt=F32):
        return nc.dram_tensor(name, list(shape), dt, kind="ExternalInput")

    def dout(name, shape, dt=F32):
        return nc.dram_tensor(name, list(shape), dt, kind="ExternalOutput")

    x_in = din("x", [NTOK, 1024]).ap()
    xs_in = din("xs", [NS, 1024]).ap()
    ck_t = din("ck", [2, NS, 2048, 512] if not KSMALL else [2, NS, 1, 512])
    cv_t = din("cv", [2, NS, 2048, 512] if not KSMALL else [2, NS, 1, 512])
    cconv_in = din("cconv", [2, NS, 30, 256]).ap()
    cpool_in = din("cpool", [2, NS, 15, 256]).ap()
    relb_in = din("relb", [32, 8]).ap()
    wgu = [din("wgu1", [2, 1024, 2 * DFF]).ap(), din("wgu2", [2, 1024, 2 * DFF]).ap()]
    wdn = [din("wdn1", [2, DFF, 1024]).ap(), din("wdn2", [2, DFF, 1024]).ap()]
    win_in = din("win", [2, 1024, 2304]).ap()
    wout_in = din("wout", [2, 1024, 1024]).ap()
    convpw_in = din("convpw", [2, 256, 256]).ap()
    poolw_in = din("poolw", [2, 4, 64, 64]).ap()
    pp_in = din("pp", [128, 232]).ap()
    cst_in = din("cst", [128, 1024]).ap()
    oh_in = din("oh", [33, 1152]).ap()
    ohs_in = din("ohs", [32, 400]).ap()
    bmask_in = din("bmask", [8, 512]).ap()
    esel_in = din("esel", [8, 256]).ap()
    convw_t = din("convw", [2, 31, 256])
    vec_t = din("vecs", [2, 4, 256])

    y_out = dout("y", [NTOK, 1024]).ap()
    ys_out = dout("ys", [NS, 1024]).ap()
    kp_out = dout("kp", [2, NTOK, 512]).ap()
    vp_out = dout("vp", [2, NTOK, 512]).ap()
    cvp_out = dout("cvp", [2, 30, 256]).ap()
    plp_out = dout("plp", [2, 15, 256]).ap()
    kso_out = dout("kso", [2, NS, 512]).ap()
    vso_out = dout("vso", [2, NS, 512]).ap()
    cvs_out = dout("cvs", [2, NS, 30, 256]).ap()
    pls_out = dout("pls", [2, NS, 15, 256]).ap()

    UQ = nc.dram_tensor("UQ", [4 * 128, NTOK], BF16).ap()
    UX_all = [[nc.dram_tensor(f"UX{l}_{i}", [4 * 128, NTOK], BF16) for i in range(3)] for l in range(2)]
    UG_all = [[nc.dram_tensor(f"UG{l}_{i}", [2 * 4 * 128, NTOK], BF16) for i in range(3)] for l in range(2)]
    cur = [0]

    def UXr(i):
        return UX_all[cur[0]][i // 4].ap()[(i % 4) * 128:(i % 4 + 1) * 128, :]

    def UGr(i, rank=0):
        return UG_all[cur[0]][i // 4].ap()[(rank * 4 + i % 4) * 128:(rank * 4 + i % 4 + 1) * 128, :]
    MIX = nc.dram_tensor("MIX", [8 * 128, NCOL], BF16).ap()
    RB_t = nc.dram_tensor("RB", [8 * 128, 1152], F32)
    RB = RB_t.ap()
    QS_t = nc.dram_tensor("QS", [NS, 512], F32)
    QS = QS_t.ap()

    off = [16640]

    def A(name, shape, dt, at=None):
        nbytes = int(np.prod(shape[1:])) * (4 if dt == F32 else 2)
        nbytes = (nbytes + 31) // 32 * 32
        if at is None:
            at = off[0]
            off[0] = at + nbytes
        return nc.alloc_sbuf_tensor_at(name, list(shape), dt, offset=at), at + nbytes

    hT, _ = A("hT", [128, 8, NCOL], F32)
    EB, _ = A("EB", [128, 24, 256], BF16)
    ident_f, _ = A("ident_f", [128, 128], F32)
    ones_f, _ = A("ones_f", [128, 128], F32)
    ident_b, _ = A("ident_b", [128, 128], BF16)
    ones_b, _ = A("ones_b", [128, 128], BF16)
    bd_b, _ = A("bd_b", [128, 128], BF16)
    on0_b, _ = A("on0_b", [128, 128], BF16)
    on1_b, _ = A("on1_b", [128, 128], BF16)
    pp, _ = A("pp", [128, 232], F32)
    cstf, _ = A("cstf", [128, 640], F32)
    SBt, _ = A("SBt", [128, 24], F32)
    rb0, _ = A("rb0", [16, 8], F32)
    bmask, _ = A("bmask", [8, 512], F32)
    esel, _ = A("esel", [8, 256], F32)
    dummy, _ = A("dummy", [128, 8], F32)
    csT, _ = A("csT", [128, 2, NS, 31], F32)
    cpT, _ = A("cpT", [128, 2, NS, 16], F32)
    glus, _ = A("glus", [NS, 256], F32)
    cs16, _ = A("cs16", [NS, 256], F32)
    S0 = off[0]
    assert S0 <= 116 * 1024, S0
    print('S0', S0)

    def FENCE():
        P.fence(lambda e: e.memset(dummy[:], 0.0))

    def gv(l, k, c):
        return pp[:, (l * 3 + k) * 8 + c:(l * 3 + k) * 8 + c + 1]
    GQ = lambda l: pp[:, 48 + l:49 + l]
    GK8 = lambda l: pp[:, 50 + l:51 + l]
    CW = lambda l, cc, w: pp[:, 52 + (l * 2 + cc) * 31 + w:53 + (l * 2 + cc) * 31 + w]
    VEC = lambda l, k, cc: pp[:, 176 + (l * 4 + k) * 2 + cc:177 + (l * 4 + k) * 2 + cc]
    HM = pp[:, 192:193]
    HB = pp[:, 193:194]
    INVW = lambda cc: pp[:, 194 + cc:195 + cc]
    ICNT = lambda cc: pp[:, 196 + cc * 16:196 + cc * 16 + 16]
    ZB = pp[:, 230:231]
    EPSC = pp[:, 231:232]

    ps = [nc.alloc_psum_tensor(f"ps{i}", [128, 512], F32) for i in range(7)]
    ps.append(nc.alloc_psum_tensor("ps7", [128, 512], F32))

    CK = "const"

    P.dma("sp", dm(pp[:], pp_in), writes=[CK])
    P.dma("sp", dm(cstf[:], cst_in[:, 0:640]), writes=["cstf"])
    P.dma("sp", dm(bmask[:], bmask_in), writes=[CK])
    P.dma("sp", dm(esel[:], esel_in), writes=[CK])
    P.op("dve", cp(ident_f[:], cstf[:, 0:128]), reads=["cstf"], writes=[CK])
    P.op("dve", cp(ones_f[:], cstf[:, 128:256]), reads=["cstf"], writes=[CK])
    P.op("dve", cp(ident_b[:], cstf[:, 0:128]), reads=["cstf"], writes=[CK])
    P.op("dve", cp(ones_b[:], cstf[:, 128:256]), reads=["cstf"], writes=[CK])
    P.op("dve", cp(bd_b[:], cstf[:, 256:384]), reads=["cstf"], writes=[CK])
    P.op("dve", cp(on0_b[:], cstf[:, 384:512]), reads=["cstf"], writes=[CK])
    P.op("dve", cp(on1_b[:], cstf[:, 512:640]), reads=["cstf"], writes=[CK])
    P.op("dve", tsc(pp[:, 48:50], pp[:, 48:50], 0.125, None, ALU.mult), reads=[CK], writes=[CK])

    sc = [S0]

    def SA(name, shape, dt):
        t, e = A(name, shape, dt, at=sc[0])
        sc[0] = e
        return t
    rbx = SA("rbx", [33, 8], F32)
    ohp = SA("ohp", [33, 1152], F32)
    ohs = SA("ohs", [32, 400], F32)
    lh = SA("lh", [33, 128], F32)
    rsb = SA("rsb", [128, 1152], F32)
    tbs = SA("tbs", [128, 3, 256], F32)
    P.op("dve", lambda e: e.memset(rbx[:], NEG), writes=["rbx"])
    P.dma("sp", dm(rbx[0:32, :], relb_in), reads=[], writes=["rbx"])
    P.dma("sp", dm(ohp[:], oh_in), writes=["ohp"])
    P.dma("sp", dm(ohs[:], ohs_in), writes=["ohs"])
    for d in range(3):
        P.op("pe", mm(ps[0][:, d * 8:(d + 1) * 8], ohs[:, d * 128:(d + 1) * 128], rbx[0:32, :]),
             reads=["ohs", "rbx"], writes=[("ps", 0)])
    P.op("dve", cp(SBt[:], ps[0][:, 0:24]), reads=[("ps", 0)], writes=[CK])
    P.op("pe", mm(ps[1][0:16, 0:8], ohs[:, 384:400], rbx[0:32, :]), reads=["ohs", "rbx"], writes=[("ps", 1)])
    P.op("dve", cp(rb0[:], ps[1][0:16, 0:8]), reads=[("ps", 1)], writes=[CK])
    for h in range(8):
        P.op("dve", tsc(lh[:], ones_f[0:33, :], rbx[:, h:h + 1], None, ALU.mult), reads=[CK, "rbx"], writes=["lh"])
        for d in range(3):
            P.op("pe", mm(ps[2 + d][:, 0:384], lh[:], ohp[:, d * 384:(d + 1) * 384]),
                 reads=["lh", "ohp"], writes=[("ps", 2 + d)])
            P.op("act", actf(rsb[:, d * 384:(d + 1) * 384], ps[2 + d][:, 0:384], AF.Copy),
                 reads=[("ps", 2 + d)], writes=["rsb"])
        P.dma("sp", dm(RB[h * 128:(h + 1) * 128, :], rsb[:]), reads=["rsb"], writes=[("RB", h)])
        for d in range(3):
            src = bass.AP(tensor=RB_t, offset=h * 128 * 1152 + d * 384 + 127, ap=[[1151, 128], [1, 256]])
            P.dma("sp", dm(tbs[:, d, :], src), reads=[("RB", h)], writes=["tbs"])
        for d in range(3):
            P.op("act", actf(EB[:, d * 8 + h, :], tbs[:, d, :], AF.Exp), reads=["tbs"], writes=[CK])

    sc[0] = S0
    xtok = [SA(f"xtok{i}", [128, 1024], F32) for i in range(2)]
    for blk in range(17):
        xb = xtok[blk % 2]
        n = 128 if blk < 16 else NS
        srcx = x_in[blk * 128:(blk + 1) * 128, :] if blk < 16 else xs_in
        P.dma("sp", dm(xb[0:n, :], srcx), writes=[("xtok", blk % 2)])
        for dc in range(8):
            bank = dc % 4
            P.op("pe", tr(ps[bank][:, 0:n], xb[0:n, dc * 128:(dc + 1) * 128], ident_f[0:n, 0:n]),
                 reads=[("xtok", blk % 2), CK], writes=[("ps", bank)])
            eng = "act" if dc % 2 == 0 else "dve"
            fn = actf(hT[:, dc, blk * 128:blk * 128 + n], ps[bank][:, 0:n], AF.Copy) if eng == "act" else \
                cp(hT[:, dc, blk * 128:blk * 128 + n], ps[bank][:, 0:n])
            ti = min(blk // 4, 4)
            P.op(eng, fn, reads=[("ps", bank)], writes=[("hT", dc, ti)])

    def tile_cols(ti):
        return TILES[ti]

    def rmsnorm(l, gi, s, xn, sq, rstd):
        cs0 = SUP0[s]
        for ti in SUPER[s]:
            c0, n = TILES[ti]
            for dc in range(8):
                sb = sq[dc % 2]
                P.op("act", actf(sb[:, 0:n], hT[:, dc, c0:c0 + n], AF.Square),
                     reads=[("hT", dc, ti)], writes=[("sq", dc % 2)])
                P.op("pe", mm(ps[6][:, 0:n], ones_b[:], sb[:, 0:n], start=(dc == 0), stop=(dc == 7)),
                     reads=[("sq", dc % 2), CK], writes=[("ps", 6)])
            KVAR = os.environ.get("KVAR", "")
            if "A" in KVAR:
                P.op("dve", cp(rstd[:, 0:n], ps[6][:, 0:n]), reads=[("ps", 6), CK], writes=["rstd"])
            else:
                P.op("act", actf(rstd[:, 0:n], ps[6][:, 0:n], AF.Sqrt, bias=EPSC, scale=1.0 / 1024),
                     reads=[("ps", 6), CK], writes=["rstd"])
            if "R" not in KVAR:
                P.op("dve", rcp(rstd[:, 0:n], rstd[:, 0:n]), reads=["rstd"], writes=["rstd"])
            for dc in range(8):
                if "B" in KVAR:
                    break
                P.op("dve", stt(xn[:, dc, c0 - cs0:c0 - cs0 + n], hT[:, dc, c0:c0 + n], gv(l, gi, dc),
                                rstd[:, 0:n], ALU.mult, ALU.mult),
                     reads=[("hT", dc, ti), "rstd", CK], writes=[("xn", dc, ti)])

    def ffn(l, which):
        gi = 0 if which == 0 else 2
        sc[0] = S0
        xn = SA("xn_f", [128, 8, SW], BF16)
        actb = SA("actb", [128, 22, SW], BF16)
        wg = [SA(f"wg{i}", [128, 8, 128], BF16) for i in range(3)]
        wu = [SA(f"wu{i}", [128, 8, 128], BF16) for i in range(3)]
        wd = [SA(f"wd{i}", [128, 22, 128], BF16) for i in range(2)]
        sq = [SA(f"sq{i}", [128, 512], BF16) for i in range(2)]
        rstd = SA("rstd", [128, 512], F32)
        sg = [SA(f"sg{i}", [128, 512], F32) for i in range(2)]
        assert sc[0] <= 229376, sc[0]
        W = wgu[which][l].rearrange("(kc p) f -> p kc f", p=128)
        Wd = wdn[which][l].rearrange("(kc p) f -> p kc f", p=128)
        cnt = 0
        for s in range(2):
            cs0 = SUP0[s]
            rmsnorm(l, gi, s, xn, sq, rstd)
            if KSUB <= 2:
                return
            for j in range(22):
                if KSUB <= 3 and j >= 1:
                    break
                b = j % 3
                P.dma("pool", dm(wg[b][:], W[:, :, j * 128:(j + 1) * 128]), writes=[("wg", b)])
                P.dma("pool", dm(wu[b][:], W[:, :, DFF + j * 128:DFF + (j + 1) * 128]), writes=[("wu", b)])
                for ti in SUPER[s]:
                    c0, n = TILES[ti]
                    lc = c0 - cs0
                    bg, bu = cnt % 2, 2 + cnt % 2
                    for kc in range(8):
                        P.op("pe", mm(ps[bg][:, 0:n], wg[b][:, kc, :], xn[:, kc, lc:lc + n], start=(kc == 0), stop=(kc == 7)),
                             reads=[("wg", b), ("xn", kc, ti)], writes=[("ps", bg)])
                    for kc in range(8):
                        P.op("pe", mm(ps[bu][:, 0:n], wu[b][:, kc, :], xn[:, kc, lc:lc + n], start=(kc == 0), stop=(kc == 7)),
                             reads=[("wu", b), ("xn", kc, ti)], writes=[("ps", bu)])
                    P.op("act", actf(sg[cnt % 2][:, 0:n], ps[bg][:, 0:n], AF.Silu), reads=[("ps", bg)], writes=[("sg", cnt % 2)])
                    P.op("dve", tt(actb[:, j, lc:lc + n], sg[cnt % 2][:, 0:n], ps[bu][:, 0:n], ALU.mult),
                         reads=[("sg", cnt % 2), ("ps", bu)], writes=[("actb", j, ti)])
                    cnt += 1
            if KSUB <= 4:
                return
            for dc in range(8):
                b = dc % 2
                P.dma("pool", dm(wd[b][:], Wd[:, :, dc * 128:(dc + 1) * 128]), writes=[("wd", b)])
                for ti in SUPER[s]:
                    c0, n = TILES[ti]
                    lc = c0 - cs0
                    bk = 4 + cnt % 2
                    for j in range(22):
                        P.op("pe", mm(ps[bk][:, 0:n], wd[b][:, j, :], actb[:, j, lc:lc + n], start=(j == 0), stop=(j == 21)),
                             reads=[("wd", b), ("actb", j, ti)], writes=[("ps", bk)])
                    P.op("dve", stt(hT[:, dc, c0:c0 + n], ps[bk][:, 0:n], 0.5, hT[:, dc, c0:c0 + n], ALU.mult, ALU.add),
                         reads=[("ps", bk), ("hT", dc, ti)], writes=[("hT", dc, ti)])
                    cnt += 1
            if KSUB <= 5:
                return

    cc_sems = []

    def mixer(l):
        sc[0] = S0
        cur[0] = l
        UX_ts, UG_ts = UX_all[l], UG_all[l]
        xn = SA("xn_m", [128, 8, SW], BF16)
        wi = [SA(f"wi{i}", [128, 8, 128], BF16) for i in range(4)]
        sq = [SA(f"sqm{i}", [128, 512], BF16) for i in range(2)]
        rstd = SA("rstdm", [128, 512], F32)
        qf = [SA(f"qf{i}", [128, 512], F32) for i in range(4)]
        sqn = [SA(f"sqn{i}", [128, 512], BF16) for i in range(2)]
        rs3 = [SA(f"rsq{i}", [128, 512], F32) for i in range(3)]
        ob = [SA(f"ob{i}", [128, 512], BF16) for i in range(3)]
        ko = [SA(f"ko{i}", [128, 512], F32) for i in range(2)]
        us_q = SA("us_q", [NS, 512], F32)
        us_k = SA("us_k", [NS, 512], F32)
        us_v = SA("us_v", [NS, 512], F32)
        tail = SA("tail", [32, 128], F32)
        stg = [SA(f"stg{i}", [120, 256], F32) for i in range(2)]
        P1_END = sc[0]
        Wi = win_in[l].rearrange("(kc p) f -> p kc f", p=128)
        P.dma("sp", dm(cvs_out[l][:, 0:29, :], cconv_in[l][:, 1:30, :]), writes=[])
        P.dma("sp", dm(pls_out[l][:, 0:14, :], cpool_in[l][:, 1:15, :]), writes=[])
        for t in range(4):
            sb_ = stg[t % 2]
            P.dma("sp", dm(sb_[:], cconv_in[l][4 * t:4 * t + 4].rearrange("b w c -> (b w) c")), writes=[("stg", t % 2)])
            for cc in range(2):
                P.op("pe", tr(ps[6][:, 0:120], sb_[:, cc * 128:(cc + 1) * 128], ident_f[0:120, 0:120]), reads=[("stg", t % 2), CK], writes=[("ps", 6)])
                P.op("dve", cp(csT[:, cc, 4 * t:4 * t + 4, 0:30], ps[6][:, 0:120].rearrange("p (b w) -> p b w", w=30)),
                     reads=[("ps", 6)], writes=["csT"])
        for t in range(2):
            sb_ = stg[t % 2]
            P.dma("sp", dm(sb_[:], cpool_in[l][8 * t:8 * t + 8].rearrange("b w c -> (b w) c")), writes=[("stg", t % 2)])
            for cc in range(2):
                P.op("pe", tr(ps[6][:, 0:120], sb_[:, cc * 128:(cc + 1) * 128], ident_f[0:120, 0:120]), reads=[("stg", t % 2), CK], writes=[("ps", 6)])
                P.op("dve", cp(cpT[:, cc, 8 * t:8 * t + 8, 0:15], ps[6][:, 0:120].rearrange("p (b w) -> p b w", w=15)),
                     reads=[("ps", 6)], writes=["cpT"])
        cnt = [0]
        kcnt = [0]

        def wload(oc):
            b = cnt[0] % 4
            cnt[0] += 1
            P.dma("pool", dm(wi[b][:], Wi[:, :, oc * 128:(oc + 1) * 128]), writes=[("wi", b)])
            return b

        def proj(b, ti, bank, cs0):
            c0, n = TILES[ti]
            lc = c0 - cs0
            for kc in range(8):
                P.op("pe", mm(ps[bank][:, 0:n], wi[b][:, kc, :], xn[:, kc, lc:lc + n], start=(kc == 0), stop=(kc == 7)),
                     reads=[("wi", b), ("xn", kc, ti)], writes=[("ps", bank)])

        def transpose_out(src, skey, ti, dst_tok, dst_cols, sample_dst, sample_key):
            if ti < 4:
                if dst_tok is None:
                    return
                kb = kcnt[0] % 2
                kcnt[0] += 1
                tb = 6 + kb
                for bq in range(4):
                    P.op("pe", tr(ps[tb][:, bq * 128:(bq + 1) * 128], src[:, bq * 128:(bq + 1) * 128], ident_f[:]),
                         reads=[skey, CK], writes=[("ps", tb)])
                P.op("act", actf(ko[kb][:], ps[tb][:], AF.Copy), reads=[("ps", tb)], writes=[("ko", kb)])
                dst = dst_tok[ti * 512:(ti + 1) * 512, dst_cols[0]:dst_cols[1]].rearrange("(b p) f -> p b f", p=128)
                P.dma("sp", dm(dst, ko[kb][:].rearrange("p (b f) -> p b f", b=4)), reads=[("ko", kb)], writes=[])
            else:
                P.op("pe", tr(ps[6][0:NS, 0:128], src[:, 0:NS], ident_f[:]), reads=[skey, CK], writes=[("ps", 6)])
                P.op("dve", cp(sample_dst, ps[6][0:NS, 0:128]), reads=[("ps", 6)], writes=[sample_key])

        for s in range(2):
            cs0 = SUP0[s]
            rmsnorm(l, 1, s, xn, sq, rstd)
            pend1 = []
            qcnt = [0]

            def qkv_post(oc, ti, bank, i3, i2):
                c0, n = TILES[ti]
                fb, obb = qf[i3], ob[i3 % 3]
                fkey, okey = ("qf", i3), ("ob", i3 % 3)
                rsb = rs3[i3 % 3]
                rkey = ("rsq", i3 % 3)
                P.op("act", actf(fb[:, 0:n], ps[bank][:, 0:n], AF.Copy), reads=[("ps", bank)], writes=[fkey])
                if oc < 8:
                    sb = sqn[i2]
                    P.op("act", actf(sb[:, 0:n], ps[bank][:, 0:n], AF.Square), reads=[("ps", bank)], writes=[("sqn", i2)])
                    P.op("pe", mm(ps[4 + i2][:, 0:n], bd_b[:], sb[:, 0:n]), reads=[("sqn", i2), CK], writes=[("ps", 4 + i2)])
                    P.op("act", actf(rsb[:, 0:n], ps[4 + i2][:, 0:n], AF.Sqrt, bias=EPSC, scale=1.0 / 64),
                         reads=[("ps", 4 + i2), CK], writes=[rkey])
                    P.op("dve", rcp(rsb[:, 0:n], rsb[:, 0:n]), reads=[rkey], writes=[rkey])
                    g = GQ(l) if oc < 4 else GK8(l)
                    P.op("dve", stt(fb[:, 0:n], fb[:, 0:n], g, rsb[:, 0:n], ALU.mult, ALU.mult),
                         reads=[fkey, rkey, CK], writes=[fkey])
                if ti < 4:
                    P.op("act", actf(obb[:, 0:n], fb[:, 0:n], AF.Copy), reads=[fkey], writes=[okey])
                    if oc < 4:
                        P.dma("sp", dm(UQ[oc * 128:(oc + 1) * 128, c0:c0 + n], obb[:, 0:n]), reads=[okey], writes=[("UQ", oc)])
                    else:
                        P.dma("sp", dm(UXr(oc - 4)[:, c0:c0 + n], obb[:, 0:n]), reads=[okey], writes=[("UX", oc - 4)])

            def qkv_post_b(oc, ti, bank, i3, i2):
                fb = qf[i3]
                fkey = ("qf", i3)
                if oc < 4:
                    transpose_out(fb, fkey, ti, None, None, us_q[:, oc * 128:(oc + 1) * 128], "us_q")
                elif oc < 8:
                    transpose_out(fb, fkey, ti, kp_out[l], ((oc - 4) * 128, (oc - 3) * 128), us_k[:, (oc - 4) * 128:(oc - 3) * 128], "us_k")
                else:
                    transpose_out(fb, fkey, ti, vp_out[l], ((oc - 8) * 128, (oc - 7) * 128), us_v[:, (oc - 8) * 128:(oc - 7) * 128], "us_v")

            pend2 = []

            def step_post(flush=False):
                if pend1 and (flush or len(pend1) > 1):
                    a = pend1.pop(0)
                    qkv_post(*a)
                    pend2.append(a)
                if pend2 and (flush or len(pend2) > 1):
                    qkv_post_b(*pend2.pop(0))

            for oc in range(12):
                b = wload(oc)
                for ti in SUPER[s]:
                    bank = cnt[0] % 4
                    cnt[0] += 1
                    proj(b, ti, bank, cs0)
                    qcnt[0] += 1
                    pend1.append((oc, ti, bank, qcnt[0] % 4, qcnt[0] % 2))
                    step_post()
            while pend1 or pend2:
                step_post(flush=True)
            for cc in range(2):
                bv = wload(12 + cc)
                bg = wload(14 + cc)
                for ti in SUPER[s]:
                    c0, n = TILES[ti]
                    bank1 = cnt[0] % 4
                    bank2 = (cnt[0] + 1) % 4
                    cnt[0] += 2
                    proj(bv, ti, bank1, cs0)
                    proj(bg, ti, bank2, cs0)
                    i3 = cnt[0] % 3
                    fb, obb = qf[i3], ob[i3]
                    fkey, okey = ("qf", i3), ("ob", i3)
                    P.op("act", actf(fb[:, 0:n], ps[bank2][:, 0:n], AF.Sigmoid), reads=[("ps", bank2)], writes=[fkey])
                    P.op("dve", tt(fb[:, 0:n], fb[:, 0:n], ps[bank1][:, 0:n], ALU.mult), reads=[fkey, ("ps", bank1)], writes=[fkey])
                    if ti < 4:
                        P.op("act", actf(obb[:, 0:n], fb[:, 0:n], AF.Copy), reads=[fkey], writes=[okey])
                        P.dma("sp", dm(UXr(8 + cc)[:, c0:c0 + n], obb[:, 0:n]), reads=[okey], writes=[("UX", 8 + cc)])
                    if ti == 3:
                        P.op("pe", tr(ps[6][0:32, 0:128], fb[:, 480:512], ident_f[:]), reads=[fkey, CK], writes=[("ps", 6)])
                        P.op("dve", cp(tail[:], ps[6][0:32, 0:128]), reads=[("ps", 6)], writes=["tail"])
                        P.dma("sp", dm(cvp_out[l][:, cc * 128:(cc + 1) * 128], tail[2:32, :]), reads=["tail"], writes=[])
                    if ti == 4:
                        P.op("dve", cp(csT[:, cc, :, 30], fb[:, 0:NS]), reads=[fkey], writes=["csT"])
                        P.op("pe", tr(ps[6][0:NS, 0:128], fb[:, 0:NS], ident_f[:]), reads=[fkey, CK], writes=[("ps", 6)])
                        P.op("dve", cp(glus[:, cc * 128:(cc + 1) * 128], ps[6][0:NS, 0:128]), reads=[("ps", 6)], writes=["glus"])
            for cc in range(2):
                b = wload(16 + cc)
                for ti in SUPER[s]:
                    c0, n = TILES[ti]
                    bank = cnt[0] % 4
                    cnt[0] += 1
                    proj(b, ti, bank, cs0)
                    i3 = cnt[0] % 3
                    fb, obb = qf[i3], ob[i3]
                    fkey, okey = ("qf", i3), ("ob", i3)
                    P.op("act", actf(fb[:, 0:n], ps[bank][:, 0:n], AF.Copy), reads=[("ps", bank)], writes=[fkey])
                    if ti < 4:
                        P.op("act", actf(obb[:, 0:n], fb[:, 0:n], AF.Copy), reads=[fkey], writes=[okey])
                        P.dma("sp", dm(UXr(10 + cc)[:, c0:c0 + n], obb[:, 0:n]), reads=[okey], writes=[("UX", 10 + cc)])
                    if ti == 3:
                        P.op("pe", tr(ps[6][0:16, 0:128], fb[:, 496:512], ident_f[:]), reads=[fkey, CK], writes=[("ps", 6)])
                        P.op("dve", cp(tail[0:16, :], ps[6][0:16, 0:128]), reads=[("ps", 6)], writes=["tail"])
                        P.dma("sp", dm(plp_out[l][:, cc * 128:(cc + 1) * 128], tail[1:16, :]), reads=["tail"], writes=[])
                    if ti == 4:
                        P.op("dve", cp(cpT[:, cc, :, 15], fb[:, 0:NS]), reads=[fkey], writes=["cpT"])
                        P.op("pe", tr(ps[6][0:NS, 0:128], fb[:, 0:NS], ident_f[:]), reads=[fkey, CK], writes=[("ps", 6)])
                        P.op("dve", cp(cs16[:, cc * 128:(cc + 1) * 128], ps[6][0:NS, 0:128]), reads=[("ps", 6)], writes=["cs16"])

        for g3 in range(3):
            sem = nc.alloc_semaphore(f"ccsem{l}_{g3}")
            cc_sems.append(sem)
            r = P.op("pool", lambda e, g3=g3: e.collective_compute("AllGather", ALU.bypass, replica_groups=[[2 * i, 2 * i + 1] for i in range(NCORES // 2)],
                                                                    ins=[UX_ts[g3].ap().opt()], outs=[UG_ts[g3].ap().opt()]),
                     reads=[("UX", i) for i in range(g3 * 4, g3 * 4 + 4)], writes=[("UG", g3)], kind="cc")
            r.sem = sem

        P.dma("sp", dm(kso_out[l], us_k[:]), reads=["us_k"], writes=[])
        P.dma("sp", dm(vso_out[l], us_v[:]), reads=["us_v"], writes=[])
        P.dma("sp", dm(cvs_out[l][:, 29, :], glus[:]), reads=["glus"], writes=[])
        P.dma("sp", dm(pls_out[l][:, 14, :], cs16[:]), reads=["cs16"], writes=[])
        P.dma("sp", dm(QS, us_q[:]), reads=["us_q"], writes=["QS"])

        if l == 0 and KSTOP < 3:
            return
        KP3 = int(os.environ.get("KP3", "99"))
        sc[0] = P1_END
        Kg = [SA(f"Kg{i}", [128, 3, 512], F32) for i in range(2)]
        Vg = [SA(f"Vg{i}", [128, 3, 512], F32) for i in range(2)]
        qb = [SA(f"qb{i}", [128, 512], F32) for i in range(2)]
        tmp = SA("stmp", [128, 3, 512], F32)
        lg = SA("lg", [128, 24], F32)
        pexp = SA("pexp", [128, 24], F32)
        Rr = SA("Rr", [8, 520], F32)
        t16 = SA("t16", [NS, 512], F32)
        l0 = SA("l0", [NS, 8], F32)
        p0 = SA("p0", [NS, 8], F32)
        dn = SA("dn", [NS, 8], F32)
        numt = SA("numt", [NS, 512], F32)
        msT = SA("msT", [128, 4, NS], BF16)
        assert sc[0] <= 229376, sc[0]
        PS_N, PS_D, PS_AN, PS_AD = 0, 1, 2, 3
        for b in range(NS if not KSMALL else 0):
            i2 = b % 2
            P.dma("sp", dm(qb[i2][:], bass.AP(tensor=QS_t, offset=b * 512, ap=[[0, 128], [1, 512]])), reads=["QS"], writes=[("qb", i2)])
            for d, dil in enumerate(DILS):
                o = ((l * NS + b) * 2048 + (2048 - 128 * dil)) * 512
                P.dma("sp", dm(Kg[i2][:, d, :], bass.AP(tensor=ck_t, offset=o, ap=[[dil * 512, 128], [1, 512]])), writes=[("Kg", i2, d)])
                P.dma("sp", dm(Vg[i2][:, d, :], bass.AP(tensor=cv_t, offset=o, ap=[[dil * 512, 128], [1, 512]])), writes=[("Vg", i2, d)])
            for d in range(3):
                P.op("dve", tt(tmp[:, d, :], Kg[i2][:, d, :], qb[i2][:], ALU.mult), reads=[("Kg", i2, d), ("qb", i2)], writes=["stmp"])
            P.op("dve", lambda e: e.tensor_reduce(out=lg[:], in_=tmp[:].rearrange("p a (h d) -> p (a h) d", d=64), axis=AX.X, op=ALU.add),
                 reads=["stmp"], writes=["lg"])
            P.op("dve", tt(lg[:], lg[:], SBt[:], ALU.add), reads=["lg", CK], writes=["lg"])
            P.op("act", actf(pexp[:], lg[:], AF.Exp), reads=["lg"], writes=["pexp"])
            for d in range(3):
                P.op("pe", mm(ps[PS_N][0:8, :], pexp[:, d * 8:(d + 1) * 8], Vg[i2][:, d, :], start=(d == 0), stop=(d == 2)),
                     reads=["pexp", ("Vg", i2, d)], writes=[("ps", PS_N)])
            for d in range(3):
                P.op("pe", mm(ps[PS_D][0:8, 0:8], pexp[:, d * 8:(d + 1) * 8], ones_f[:, 0:8], start=(d == 0), stop=(d == 2)),
                     reads=["pexp", CK], writes=[("ps", PS_D)])
            P.op("dve", tt(Rr[:, 0:512], ps[PS_N][0:8, :], bmask[:], ALU.mult), reads=[("ps", PS_N), CK], writes=["Rr"])
            P.op("dve", tt(Rr[:, 512:520], ps[PS_D][0:8, 0:8], ident_f[0:8, 0:8], ALU.mult), reads=[("ps", PS_D), CK], writes=["Rr"])
            P.op("pe", mm(ps[PS_AN][0:NS, :], esel[:, b * 16:(b + 1) * 16], Rr[:, 0:512], start=(b == 0), stop=(b == NS - 1)),
                 reads=["Rr", CK], writes=[("ps", PS_AN)])
            P.op("pe", mm(ps[PS_AD][0:NS, 0:8], esel[:, b * 16:(b + 1) * 16], Rr[:, 512:520], start=(b == 0), stop=(b == NS - 1)),
                 reads=["Rr", CK], writes=[("ps", PS_AD)])
        P.op("dve", tt(t16[:], us_q[:], us_k[:], ALU.mult), reads=["us_q", "us_k"], writes=["t16"])
        P.op("dve", lambda e: e.tensor_reduce(out=l0[:], in_=t16[:].rearrange("p (h d) -> p h d", d=64), axis=AX.X, op=ALU.add),
             reads=["t16"], writes=["l0"])
        P.op("dve", tt(l0[:], l0[:], rb0[:], ALU.add), reads=["l0", CK], writes=["l0"])
        P.op("act", actf(p0[:], l0[:], AF.Exp), reads=["l0"], writes=["p0"])
        P.op("dve", tsc(p0[:], p0[:], 3.0, None, ALU.mult), reads=["p0"], writes=["p0"])
        P.op("dve", tt(dn[:], ps[PS_AD][0:NS, 0:8], p0[:], ALU.add), reads=[("ps", PS_AD), "p0"], writes=["dn"])
        P.op("dve", lambda e: e.reciprocal(out=dn[:], in_=dn[:]), reads=["dn"], writes=["dn"])
        for h in range(8):
            P.op("dve", stt(numt[:, h * 64:(h + 1) * 64], us_v[:, h * 64:(h + 1) * 64], p0[:, h:h + 1],
                            ps[PS_AN][0:NS, h * 64:(h + 1) * 64], ALU.mult, ALU.add),
                 reads=["us_v", "p0", ("ps", PS_AN)], writes=[("numt", h)])
        for h in range(8):
            P.op("dve", tsc(numt[:, h * 64:(h + 1) * 64], numt[:, h * 64:(h + 1) * 64], dn[:, h:h + 1], None, ALU.mult),
                 reads=[("numt", h), "dn"], writes=[("numt", h)])
        for c in range(4):
            P.op("pe", tr(ps[4][:, 0:NS], numt[:, c * 128:(c + 1) * 128], ident_f[0:NS, 0:NS]),
                 reads=[("numt", 2 * c), ("numt", 2 * c + 1), CK], writes=[("ps", 4)])
            P.op("dve", cp(msT[:, c, :], ps[4][:, 0:NS]), reads=[("ps", 4)], writes=["msT"])
        for c in range(4):
            P.dma("sp", dm(MIX[c * 128:(c + 1) * 128, NTOK:NCOL], msT[:, c, :]), reads=["msT"], writes=[("MIXs", c)])
        FENCE()
        if l == 0 and KSTOP < 4:
            return

        sc[0] = S0
        qn = SA("qn", [128, NTOK], BF16)
        kst = SA("kst", [128, 2 * NTOK], BF16)
        vst = SA("vst", [128, 2 * NTOK], BF16)
        q4 = SA("q4", [128, 4, 512], BF16)
        q16 = SA("q16", [128, 16, 128], BF16)
        k4 = SA("k4", [128, 4, 640], BF16)
        k16 = SA("k16", [128, 16, 256], BF16)
        varr = SA("varr", [128, 4096], BF16)
        Vp0 = SA("Vp0", [128, 32, 128], BF16)
        Vp1 = SA("Vp1", [128, 32, 128], BF16)
        et = [SA(f"et{i}", [128, 256], BF16) for i in range(6)]
        pt = [SA(f"pt{i}", [128, 256], BF16) for i in range(6)]
        accn = SA("accn", [128, NTOK], F32)
        accd = SA("accd", [128, NTOK], F32)
        ao = SA("ao", [128, NTOK], BF16)
        assert sc[0] <= 229376, sc[0]
        P.op("pool", lambda e: e.memset(Vp0[:], 0.0), reads=[], writes=["Vp0"])
        P.op("pool", lambda e: e.memset(Vp1[:], 0.0), reads=[], writes=["Vp1"])
        scnt = [0]
        gcnt = [0]
        for c in range(4):
            P.dma("sp", dm(qn[:], UQ[c * 128:(c + 1) * 128, :]), writes=["qn"])
            P.dma("sp", dm(kst[:, 0:NTOK], UGr(c)), writes=["kst"])
            P.dma("sp", dm(kst[:, NTOK:], UXr(c)), writes=["kst"])
            P.dma("sp", dm(vst[:, 0:NTOK], UGr(4 + c)), writes=["vst"])
            P.dma("sp", dm(vst[:, NTOK:], UXr(4 + c)), writes=["vst"])
            P.op("pool", cp(q4[:], qn[:].rearrange("p (m r) -> p r m", r=4)), reads=["qn"], writes=["q4"])
            P.op("pool", cp(q16[:], qn[:].rearrange("p (m r) -> p r m", r=16)), reads=["qn"], writes=["q16"])
            P.op("dve", cp(k4[:], kst[:, 1536:4096].rearrange("p (m r) -> p r m", r=4)), reads=["kst"], writes=["k4"])
            P.op("dve", cp(k16[:], kst[:].rearrange("p (m r) -> p r m", r=16)), reads=["kst"], writes=["k16"])
            for di, dil in enumerate(DILS):
                nres = dil
                nqb = 16 // dil
                nkb = nqb + 1
                if dil == 1:
                    karr = lambda rho, kb: kst[:, 1920 + kb * 128:1920 + (kb + 1) * 128]
                    qarr = lambda rho, lo, hi: qn[:, lo:hi]
                    vsrc = lambda rho, kb: vst[:, 1920 + kb * 128:1920 + (kb + 1) * 128]
                    vkey, kkey, qkey = "vst", "kst", "qn"
                elif dil == 4:
                    karr = lambda rho, kb: k4[:, rho, kb * 128:(kb + 1) * 128]
                    qarr = lambda rho, lo, hi: q4[:, rho, lo:hi]
                    P.op("pool", cp(varr[:, 0:2560].rearrange("p (r m) -> p r m", r=4), vst[:, 1536:4096].rearrange("p (m r) -> p r m", r=4)),
                         reads=["vst"], writes=["varr"])
                    vsrc = lambda rho, kb: varr[:, rho * 640 + kb * 128:rho * 640 + (kb + 1) * 128]
                    vkey, kkey, qkey = "varr", "k4", "q4"
                else:
                    karr = lambda rho, kb: k16[:, rho, kb * 128:(kb + 1) * 128]
                    qarr = lambda rho, lo, hi: q16[:, rho, lo:hi]
                    P.op("pool", cp(varr[:].rearrange("p (r m) -> p r m", r=16), vst[:].rearrange("p (m r) -> p r m", r=16)),
                         reads=["vst"], writes=["varr"])
                    vsrc = lambda rho, kb: varr[:, rho * 256 + kb * 128:rho * 256 + (kb + 1) * 128]
                    vkey, kkey, qkey = "varr", "k16", "q16"
                blks = [(rho, kb) for rho in range(nres) for kb in range(nkb)]
                if KP3 <= 1:
                    continue
                KDIL = os.environ.get("KDIL", "")
                if KDIL and str(di) not in KDIL:
                    continue
                for g0 in range(0, len(blks), 4):
                    grp = blks[g0:g0 + 4]
                    half = (g0 // 4) % 2
                    if os.environ.get("KNOBC", "0") == "1":
                        half = 0
                    pbank = ps[7 - half][:].bitcast(BF16)[:, 0:512]
                    for i, (rho, kb) in enumerate(grp):
                        P.op("pe", tr(pbank[:, i * 128:(i + 1) * 128], vsrc(rho, kb), ident_b[:]),
                             reads=[vkey, CK], writes=[("pss", 3 - half)])
                    ng = len(grp)
                    src = pbank[:, 0:ng * 128].rearrange("p (b f) -> p b f", b=ng)
                    KEV = os.environ.get("KEV", "dD")
                    if "a" in KEV:
                        P.op("act", actf(Vp0[:, g0:g0 + ng, 0:64], src[:, :, 0:64], AF.Copy), reads=[("pss", 3 - half)], writes=["Vp0"])
                    if "d" in KEV:
                        P.op("dve", cp(Vp1[:, g0:g0 + ng, 64:128], src[:, :, 64:128]), reads=[("pss", 3 - half)], writes=["Vp1"])
                    if "D" in KEV:
                        P.op("dve", cp(Vp0[:, g0:g0 + ng, 0:64], src[:, :, 0:64]), reads=[("pss", 3 - half)], writes=["Vp0"])
                if KP3 <= 2:
                    continue
                if dil == 1:
                    groups = [[(0, q) for q in range(g * 4, g * 4 + 4)] for g in range(4)]
                elif dil == 4:
                    groups = [[(rho, q) for q in range(4)] for rho in range(4)]
                else:
                    groups = [[(rho, 0) for rho in range(g * 4, g * 4 + 4)] for g in range(4)]
                qloc = {}
                for gi_, grp in enumerate(groups):
                    for i, rq in enumerate(grp):
                        qloc[rq] = (gi_, i)
                gbank = {}
                def pv_part(rho, kb, qbs, lo, blk, pts):
                    for q in qbs:
                        gi_, i = qloc[(rho, q)]
                        if gi_ not in gbank:
                            gbank[gi_] = gcnt[0] % 2
                            gcnt[0] += 1
                        gb = gbank[gi_]
                        xo = (q * 128 - lo)
                        is_prev = (kb == q)
                        for hh in range(2):
                            st = is_prev and hh == 0
                            sp_ = (not is_prev) and hh == 1
                            Vp = Vp0 if hh == 0 else Vp1
                            on = on0_b if hh == 0 else on1_b
                            P.op("pe", mm(ps[2 + gb][:, i * 128:(i + 1) * 128], Vp[:, blk, :], pt[pts[hh]][:, xo:xo + 128], start=st, stop=sp_),
                                 reads=["Vp0" if hh == 0 else "Vp1", ("pt", pts[hh])], writes=[("ps", 2 + gb)])
                            P.op("pe", mm(ps[4 + gb][:, i * 128:(i + 1) * 128], on[:], pt[pts[hh]][:, xo:xo + 128], start=st, stop=sp_),
                                 reads=[CK, ("pt", pts[hh])], writes=[("ps", 4 + gb)])
                        if (not is_prev) and i == len(groups[gi_]) - 1 and KP3 <= 4:
                            del gbank[gi_]
                        elif (not is_prev) and i == len(groups[gi_]) - 1:
                            if dil == 1:
                                oa = lambda a, g=gi_: a[:, g * 512:(g + 1) * 512]
                                pin = lambda b_: ps[b_][:, :]
                            elif dil == 4:
                                oa = lambda a, r=gi_: a[:].rearrange("p (m r) -> p m r", r=4)[:, :, r]
                                pin = lambda b_: ps[b_][:, :]
                            else:
                                oa = lambda a, g=gi_: a[:].rearrange("p (m r) -> p m r", r=16)[:, :, g * 4:(g + 1) * 4]
                                pin = lambda b_: ps[b_][:, :].rearrange("p (r m) -> p m r", r=4)
                            if dil == 1:
                                P.op("act", actf(oa(accn), pin(2 + gb), AF.Copy), reads=[("ps", 2 + gb)], writes=["accn"])
                                P.op("dve", cp(oa(accd), pin(4 + gb)), reads=[("ps", 4 + gb)], writes=["accd"])
                            else:
                                P.op("dve", tt(oa(accn), oa(accn), pin(2 + gb), ALU.add), reads=[("ps", 2 + gb), "accn"], writes=["accn"])
                                P.op("dve", tt(oa(accd), oa(accd), pin(4 + gb), ALU.add), reads=[("ps", 4 + gb), "accd"], writes=["accd"])
                            del gbank[gi_]
                pend = []
                for rho in range(nres):
                    for kb in range(nkb):
                        qbs = [q for q in (kb - 1, kb) if 0 <= q < nqb]
                        lo, hi = qbs[0] * 128, (qbs[-1] + 1) * 128
                        x0 = 0 if (kb - 1) >= 0 else 128
                        N = hi - lo
                        pts = []
                        for hh in range(2):
                            pbse = hh * 64
                            si = scnt[0] % 4
                            scnt[0] += 1
                            spsum = ps[(0, 1, 6, 7)[si]][:, 0:N]
                            P.op("pe", mm(spsum, karr(rho, kb)[pbse:pbse + 64, :], qarr(rho, lo, hi)[pbse:pbse + 64, :]),
                                 reads=[kkey, qkey], writes=[("pss", si)])
                            ei = scnt[0] % 6
                            bias = HB if kb == 0 else ZB
                            P.op("act", actf(et[ei][:, 0:N], spsum, AF.Exp, bias=bias), reads=[("pss", si), CK], writes=[("et", ei)])
                            P.op("dve", tt(pt[ei][:, 0:N], et[ei][:, 0:N], EB[:, di * 8 + 2 * c + hh, x0:x0 + N], ALU.mult),
                                 reads=[("et", ei), CK], writes=[("pt", ei)])
                            pts.append(ei)
                        blk = rho * nkb + kb
                        if KP3 <= 3:
                            continue
                        pend.append((rho, kb, qbs, lo, blk, pts))
                        if len(pend) > 1:
                            pv_part(*pend.pop(0))
                while pend:
                    pv_part(*pend.pop(0))
            P.op("dve", lambda e: e.reciprocal(out=accd[:], in_=accd[:]), reads=["accd"], writes=["accd"])
            P.op("dve", tt(ao[:], accn[:], accd[:], ALU.mult), reads=["accn", "accd"], writes=["ao"])
            P.dma("sp", dm(MIX[c * 128:(c + 1) * 128, 0:NTOK], ao[:]), reads=["ao"], writes=[("MIXp", c)])
        FENCE()
        if l == 0 and KSTOP < 5:
            return

        sc[0] = S0
        gl = [SA(f"gl{i}", [128, 30 + NTOK], BF16) for i in range(2)]
        yacc = [SA(f"yacc{i}", [128, NCOL], F32) for i in range(2)]
        sqf = [SA(f"sqf{i}", [128, 512], F32) for i in range(2)]
        rsc = SA("rsc", [128, 512], F32)
        zb = SA("zb", [128, 2, 512], BF16)
        cob = [SA(f"cob{i}", [128, 512], BF16) for i in range(2)]
        pwb2 = SA("pwb2", [128, 2, 256], BF16)
        pbd2 = [SA(f"pbd2{i}", [128, 128], BF16) for i in range(2)]
        ce = [SA(f"ce{i}", [128, 16 + NTOK], F32) for i in range(2)]
        s2 = SA("s2", [128, 16 + NTOK], F32)
        s4 = SA("s4", [128, 16 + NTOK], F32)
        dpl = SA("dpl", [128, NCOL], BF16)
        ctm = SA("ctm", [128, NS, 31], F32)
        cfx = SA("cfx", [128, 16], F32)
        assert sc[0] <= 229376, sc[0]
        P.dma("pool", dm(pwb2[:], convpw_in[l].rearrange("(kc p) f -> p kc f", p=128)), writes=["pwb2"])
        for cc in range(2):
            P.op("pool", lambda e, cc=cc: e.memset(pbd2[cc][:], 0.0), writes=[("pbd2", cc)])
            for gg in range(2):
                P.dma("pool", dm(pbd2[cc][gg * 64:(gg + 1) * 64, gg * 64:(gg + 1) * 64], poolw_in[l, cc * 2 + gg]), writes=[("pbd2", cc)])
        for cc in range(2):
            P.dma("sp", dm(gl[cc][:, 0:30], UGr(8 + cc)[:, NTOK - 30:NTOK]), writes=[("gl", cc)])
            P.dma("sp", dm(gl[cc][:, 30:], UXr(8 + cc)), writes=[("gl", cc)])
            P.op("dve", tsc(gl[cc][:, 0:30], gl[cc][:, 0:30], HM, None, ALU.mult), reads=[("gl", cc), CK], writes=[("gl", cc)])
            eng = "dve"
            P.op(eng, tsc(yacc[cc][:, 0:NTOK], gl[cc][:, 0:NTOK], CW(l, cc, 0), VEC(l, 0, cc), ALU.mult, ALU.add),
                 reads=[("gl", cc), CK], writes=[("yacc", cc)])
            for w in range(1, 31):
                P.op(eng, stt(yacc[cc][:, 0:NTOK], gl[cc][:, w:w + NTOK], CW(l, cc, w), yacc[cc][:, 0:NTOK], ALU.mult, ALU.add),
                     reads=[("gl", cc), CK, ("yacc", cc)], writes=[("yacc", cc)])
            cwv = pp[:, 52 + (l * 2 + cc) * 31:52 + (l * 2 + cc) * 31 + 31]
            for b in range(NS):
                P.op("dve", tt(ctm[:, b, :], csT[:, cc, b, :], cwv, ALU.mult), reads=["csT", CK], writes=["ctm"])
            P.op("dve", lambda e, cc=cc: e.tensor_reduce(out=yacc[cc][:, NTOK:NCOL], in_=ctm[:], axis=AX.X, op=ALU.add),
                 reads=["ctm"], writes=[("yaccs", cc)])
            P.op("dve", tsc(yacc[cc][:, NTOK:NCOL], yacc[cc][:, NTOK:NCOL], VEC(l, 0, cc), None, ALU.add), reads=[("yaccs", cc), CK], writes=[("yaccs", cc)])
        for ti in range(5):
            c0, n = TILES[ti]
            yk = (lambda cc: ("yacc", cc)) if ti < 4 else (lambda cc: ("yaccs", cc))
            for cc in range(2):
                P.op("pe", mm(ps[0][:, 0:n], ones_f[:], yacc[cc][:, c0:c0 + n], start=(cc == 0), stop=(cc == 1)),
                     reads=[yk(cc), CK], writes=[("ps", 0)])
            for cc in range(2):
                P.op("dve", stt(yacc[cc][:, c0:c0 + n], ps[0][:, 0:n], -1.0 / 256, yacc[cc][:, c0:c0 + n], ALU.mult, ALU.add),
                     reads=[("ps", 0), yk(cc)], writes=[yk(cc)])
                P.op("act", actf(sqf[cc][:, 0:n], yacc[cc][:, c0:c0 + n], AF.Square), reads=[yk(cc)], writes=[("sqf", cc)])
            for cc in range(2):
                P.op("pe", mm(ps[1][:, 0:n], ones_f[:], sqf[cc][:, 0:n], start=(cc == 0), stop=(cc == 1)),
                     reads=[("sqf", cc), CK], writes=[("ps", 1)])
            P.op("act", actf(rsc[:, 0:n], ps[1][:, 0:n], AF.Sqrt, bias=EPSC, scale=1.0 / 256), reads=[("ps", 1), CK], writes=["rsc"])
            P.op("dve", rcp(rsc[:, 0:n], rsc[:, 0:n]), reads=["rsc"], writes=["rsc"])
            for cc in range(2):
                P.op("dve", tt(sqf[cc][:, 0:n], yacc[cc][:, c0:c0 + n], rsc[:, 0:n], ALU.mult), reads=[yk(cc), "rsc"], writes=[("sqf", cc)])
                P.op("dve", tsc(sqf[cc][:, 0:n], sqf[cc][:, 0:n], VEC(l, 1, cc), VEC(l, 2, cc), ALU.mult, ALU.add),
                     reads=[("sqf", cc), CK], writes=[("sqf", cc)])
                P.op("act", actf(zb[:, cc, 0:n], sqf[cc][:, 0:n], AF.Silu), reads=[("sqf", cc)], writes=[("zb", cc)])
            for co in range(2):
                for ci in range(2):
                    P.op("pe", mm(ps[2 + co][:, 0:n], pwb2[:, ci, co * 128:(co + 1) * 128], zb[:, ci, 0:n], start=(ci == 0), stop=(ci == 1)),
                         reads=["pwb2", ("zb", ci)], writes=[("ps", 2 + co)])
                P.op("act", actf(cob[co][:, 0:n], ps[2 + co][:, 0:n], AF.Copy), reads=[("ps", 2 + co)], writes=[("cob", co)])
                P.dma("sp", dm(MIX[(4 + co) * 128:(5 + co) * 128, c0:c0 + n], cob[co][:, 0:n]), reads=[("cob", co)], writes=[("MIXc", co, ti)])
        W_ = 16 + NTOK
        for cc in range(2):
            x = ce[cc]
            P.dma("sp", dm(dpl[:, 0:15], UGr(10 + cc)[:, NTOK - 15:NTOK]), writes=["dpl"])
            P.op("dve", lambda e, cc=cc: e.memset(ce[cc][:, 0:1], 0.0), writes=[("ce", cc)])
            P.op("dve", tsc(x[:, 1:16], dpl[:, 0:15], HM, None, ALU.mult), reads=["dpl", CK], writes=[("ce", cc)])
            P.dma("sp", dm(dpl[:, 0:NTOK], UXr(10 + cc)), reads=[("ce", cc)], writes=["dpl"])
            P.op("dve", cp(x[:, 16:], dpl[:, 0:NTOK]), reads=["dpl"], writes=[("ce", cc)])
            P.op("dve", tt(s2[:, 1:W_], x[:, 1:W_], x[:, 0:W_ - 1], ALU.add), reads=[("ce", cc)], writes=["s2"])
            P.op("dve", tt(s4[:, 3:W_], s2[:, 3:W_], s2[:, 1:W_ - 2], ALU.add), reads=["s2"], writes=["s4"])
            if cc == 1:
                P.op("dve", tt(s2[:, 7:W_], s4[:, 7:W_], s4[:, 3:W_ - 4], ALU.add), reads=["s4", "s2"], writes=["s2"])
                P.op("dve", tt(s4[:, 15:W_], s2[:, 15:W_], s2[:, 7:W_ - 8], ALU.add), reads=["s2", "s4"], writes=["s4"])
            for half, srcs in ((0, s2), (1, s4)):
                pr = slice(half * 64, (half + 1) * 64)
                w = (2, 4, 8, 16)[cc * 2 + half]
                P.op("dve", stt(dpl[pr, 0:NTOK], srcs[pr, 16:], pp[pr, 194 + cc:195 + cc], x[pr, 16:], ALU.mult, ALU.subtract),
                     reads=["s2", "s4", ("ce", cc), CK, "dpl"], writes=["dpl"])
                P.op("dve", tt(cfx[pr, :], srcs[pr, 16:32], ICNT(cc)[pr, :], ALU.mult), reads=["s2", "s4", CK], writes=["cfx"])
                P.op("dve", tt(dpl[pr, 0:16], cfx[pr, :], x[pr, 16:32], ALU.subtract), reads=["cfx", ("ce", cc), "dpl"], writes=["dpl"])
                P.op("dve", lambda e, pr=pr, cc=cc, w=w: e.tensor_reduce(out=cfx[pr, :], in_=cpT[pr, cc, :, 16 - w:16], axis=AX.X, op=ALU.add),
                     reads=["cpT", "cfx", "dpl"], writes=["cfx"])
                P.op("dve", stt(dpl[pr, NTOK:NCOL], cfx[pr, :], 1.0 / w, cpT[pr, cc, :, 15], ALU.mult, ALU.subtract),
                     reads=["cfx", "cpT", "dpl"], writes=["dpl"])
            for ti in range(5):
                c0, n = TILES[ti]
                P.op("pe", mm(ps[4][:, 0:n], pbd2[cc][:], dpl[:, c0:c0 + n]), reads=[("pbd2", cc), "dpl"], writes=[("ps", 4)])
                P.op("dve", tsc(cob[ti % 2][:, 0:n], ps[4][:, 0:n], VEC(l, 3, cc), None, ALU.mult), reads=[("ps", 4), CK], writes=[("cob", ti % 2)])
                P.dma("sp", dm(MIX[(6 + cc) * 128:(7 + cc) * 128, c0:c0 + n], cob[ti % 2][:, 0:n]), reads=[("cob", ti % 2)], writes=[("MIXq", cc, ti)])
        FENCE()
        if l == 0 and KSTOP < 6:
            return

        sc[0] = S0
        mixT = SA("mixT", [128, 8, NCOL], BF16)
        wo = [SA(f"wo{i}", [128, 8, 128], BF16) for i in range(2)]
        for c in range(8):
            P.dma("sp", dm(mixT[:, c, :], MIX[c * 128:(c + 1) * 128, :]), writes=[("mixT", c)])
        Wo = wout_in[l].rearrange("(kc p) f -> p kc f", p=128)
        cnt2 = 0
        for dc in range(8):
            b = dc % 2
            P.dma("pool", dm(wo[b][:], Wo[:, :, dc * 128:(dc + 1) * 128]), writes=[("wo", b)])
            for ti in range(5):
                c0, n = TILES[ti]
                bk = cnt2 % 2
                cnt2 += 1
                for kc in range(8):
                    P.op("pe", mm(ps[bk][:, 0:n], wo[b][:, kc, :], mixT[:, kc, c0:c0 + n], start=(kc == 0), stop=(kc == 7)),
                         reads=[("wo", b), ("mixT", kc)], writes=[("ps", bk)])
                P.op("dve", tt(hT[:, dc, c0:c0 + n], ps[bk][:, 0:n], hT[:, dc, c0:c0 + n], ALU.add),
                     reads=[("ps", bk), ("hT", dc, ti)], writes=[("hT", dc, ti)])

    FENCE()
    for l in range(2):
        if l == 0 and KSTOP < 1:
            break
        ffn(l, 0)
        FENCE()
        if l == 0 and KSTOP < 2:
            break
        mixer(l)
        FENCE()
        if l == 0 and KSTOP < 7:
            break
        ffn(l, 1)
        FENCE()
        if l == 0 and KSTOP < 8:
            break

    sc[0] = S0
    yt = [SA(f"yt{i}", [128, 1024], F32) for i in range(2)]
    for blk in range(17):
        n = 128 if blk < 16 else NS
        yb = yt[blk % 2]
        ti = min(blk // 4, 4)
        for g in range(2):
            for dd in range(4):
                dc = g * 4 + dd
                P.op("pe", tr(ps[g][0:n, dd * 128:(dd + 1) * 128], hT[:, dc, blk * 128:blk * 128 + n], ident_f[:]),
                     reads=[("hT", dc, ti), CK], writes=[("ps", g)])
            if g == 0:
                P.op("act", actf(yb[0:n, 0:512], ps[0][0:n, :], AF.Copy), reads=[("ps", 0)], writes=[("yt", blk % 2)])
            else:
                P.op("dve", cp(yb[0:n, 512:1024], ps[1][0:n, :]), reads=[("ps", 1)], writes=[("yt", blk % 2)])
        dst = y_out[blk * 128:(blk + 1) * 128, :] if blk < 16 else ys_out
        P.dma("sp", dm(dst, yb[0:n, :]), reads=[("yt", blk % 2)], writes=[])

    emit(nc, P, cc_sems)
    return nc


def emit(nc, P, cc_sems):
    engs = Prog.ENG
    for e in engs:
        for r in P.recs[e]:
            for d in r.deps:
                if d.kind == "c" and not (d.eng == "pe" and r.eng == "pe" and r.kind == "c"):
                    d.inc = True
    cnt = {e: 0 for e in engs}
    for e in engs:
        for r in P.recs[e]:
            if r.kind == "c" and r.inc:
                cnt[e] += 1
                r.val = cnt[e]
    S = {e: nc.alloc_semaphore(f"S_{e}") for e in engs}
    NQ = {"sp": 12, "pool": 6, "act": 2, "pe": 1, "dve": 1}
    dsem = {e: [nc.alloc_semaphore(f"D_{e}{i}") for i in range(NQ[e])] for e in engs}
    dcount = {e: [0] * NQ[e] for e in engs}
    dlast = {e: [None] * NQ[e] for e in engs}
    dk = {e: 0 for e in engs}
    for e in engs:
        for r in P.recs[e]:
            if r.kind == "d":
                i = dk[e] % NQ[e]
                dk[e] += 1
                r.sem = dsem[e][i]
                dcount[e][i] += 16
                r.dval = dcount[e][i]
                if dlast[e][i] is not None:
                    r.deps.append(dlast[e][i])
                dlast[e][i] = r
            elif r.kind == "cc":
                r.dval = 1

    def run(engname, eng):
        waited = {}
        for r in P.recs[engname]:
            for d in r.deps:
                if d.kind == "c":
                    if d.eng == "pe" and engname == "pe" and r.kind == "c":
                        continue
                    key, sem, val = ("S", d.eng), S[d.eng], d.val
                else:
                    key, sem, val = ("D", id(d.sem)), d.sem, d.dval
                if waited.get(key, 0) >= val:
                    continue
                eng.wait_ge(sem, val)
                waited[key] = val
            ins = r.fn(eng)
            if r.kind == "d":
                ins.then_inc(r.sem, 16)
            elif r.kind == "cc":
                ins.then_inc(r.sem)
            elif r.inc:
                ins.then_inc(S[engname], 1)
        if engname == "sp":
            for e in engs:
                for i in range(NQ[e]):
                    if dcount[e][i] > 0:
                        eng.wait_ge(dsem[e][i], dcount[e][i])

    with nc.Block() as block:
        @block.tensor
        def _(e):
            run("pe", e)

        @block.scalar
        def _(e):
            run("act", e)

        @block.vector
        def _(e):
            run("dve", e)

        @block.gpsimd
        def _(e):
            run("pool", e)

        @block.sync
        def _(e):
            run("sp", e)


_NC = None


def _consts():
    cst = np.zeros((128, 1024), np.float32)
    cst[:, 0:128] = np.eye(128, dtype=np.float32)
    cst[:, 128:256] = 1.0
    bd = np.zeros((128, 128), np.float32)
    bd[:64, :64] = 1.0
    bd[64:, 64:] = 1.0
    cst[:, 256:384] = bd
    cst[:, 384:448] = 1.0
    cst[:, 512 + 64:640] = 1.0
    oh = np.zeros((33, 1152), np.float32)
    for d, dil in enumerate(DILS):
        for z in range(384):
            dl = z - 127
            if 0 <= dl <= 128:
                oh[int(_bucket_np(dil * dl)), d * 384 + z] = 1.0
            else:
                oh[32, d * 384 + z] = 1.0
    ohs = np.zeros((32, 400), np.float32)
    for d, dil in enumerate(DILS):
        for i in range(128):
            ohs[int(_bucket_np(dil * (128 - i))), d * 128 + i] = 1.0
    ohs[0, 384:400] = 1.0
    bmask = np.zeros((8, 512), np.float32)
    for h in range(8):
        bmask[h, h * 64:(h + 1) * 64] = 1.0
    esel = np.zeros((8, 256), np.float32)
    for b in range(16):
        esel[:, b * 16 + b] = 1.0
    return cst, oh, ohs, bmask, esel


def kernel(x_prompt, x_sample, cache_attn_k, cache_attn_v, cache_conv, cache_pool, rel_bias,
           g_ffn1, w_ffn1_gu, w_ffn1_down, g_mix, w_in, g_q, g_k, conv_w, conv_b, conv_ln_g,
           conv_ln_b, conv_pw, pool_w, pool_scale, w_out, g_ffn2, w_ffn2_gu, w_ffn2_down, _prep_only=False):
    global _NC
    f = lambda a: np.ascontiguousarray(np.asarray(a, dtype=np.float32))
    x_prompt, x_sample = f(x_prompt), f(x_sample)
    ck, cv = np.asarray(cache_attn_k), np.asarray(cache_attn_v)
    cst, oh, ohs, bmask, esel = _consts()
    pp = np.zeros((128, 232), np.float32)
    gs = [f(g_ffn1), f(g_mix), f(g_ffn2)]
    for l in range(2):
        for k in range(3):
            pp[:, (l * 3 + k) * 8:(l * 3 + k) * 8 + 8] = gs[k][l].reshape(8, 128).T
        pp[:, 48 + l] = np.tile(f(g_q)[l], 2)
        pp[:, 50 + l] = np.tile(f(g_k)[l], 2)
        for cc in range(2):
            pp[:, 52 + (l * 2 + cc) * 31:52 + (l * 2 + cc) * 31 + 31] = f(conv_w)[l][:, cc * 128:(cc + 1) * 128].T
            for k, v in enumerate((conv_b, conv_ln_g, conv_ln_b, pool_scale)):
                pp[:, 176 + (l * 4 + k) * 2 + cc] = f(v)[l][cc * 128:(cc + 1) * 128]
    vecs = np.stack([np.stack([f(conv_b)[l], f(conv_ln_g)[l], f(conv_ln_b)[l], f(pool_scale)[l]]) for l in range(2)])
    wins = np.array([2, 4, 8, 16])
    in_maps = []
    for i in range(8):
        b, half = i // 2, i % 2
        ppi = pp.copy()
        ppi[:, 231] = EPS
        ppi[:, 192] = 1.0 if half else 0.0
        ppi[:, 193] = 0.0 if half else NEG
        for cc in range(2):
            w = np.repeat(wins[cc * 2:cc * 2 + 2], 64).astype(np.float32)
            ppi[:, 194 + cc] = 1.0 / w
            for t in range(16):
                ppi[:, 196 + cc * 16 + t] = 1.0 / np.minimum(w, half * NTOK + t + 1)
        sl = slice(i * NS, (i + 1) * NS)
        in_maps.append({
            "x": np.ascontiguousarray(x_prompt[b, half * NTOK:(half + 1) * NTOK]),
            "xs": np.ascontiguousarray(x_sample[sl, 0]),
            "ck": np.ascontiguousarray(ck[:, sl].reshape(2, NS, -1, 512), dtype=np.float32),
            "cv": np.ascontiguousarray(cv[:, sl].reshape(2, NS, -1, 512), dtype=np.float32),
            "cconv": np.ascontiguousarray(f(cache_conv)[:, sl]),
            "cpool": np.ascontiguousarray(f(cache_pool)[:, sl]),
            "relb": f(rel_bias),
            "wgu1": f(w_ffn1_gu), "wgu2": f(w_ffn2_gu), "wdn1": f(w_ffn1_down), "wdn2": f(w_ffn2_down),
            "win": f(w_in), "wout": f(w_out), "convpw": f(conv_pw), "poolw": f(pool_w),
            "pp": ppi, "cst": cst, "oh": oh, "ohs": ohs, "bmask": bmask, "esel": esel,
            "convw": f(conv_w), "vecs": vecs,
        })
    if _prep_only:
        return in_maps
    if _NC is None:
        _NC = build_program()
    res = run_bass_kernel_spmd(_NC, in_maps, core_ids=list(range(8)))
    return assemble(res.results)


def assemble(R, ncores=8):
    y_prompt = np.zeros((4, 4096, 1024), np.float32)
    y_sample = np.zeros((128, 1, 1024), np.float32)
    nkp = np.zeros((2, 4, 2048, 8, 64), np.float32)
    nvp = np.zeros((2, 4, 2048, 8, 64), np.float32)
    ncp = np.zeros((2, 4, 30, 256), np.float32)
    npp = np.zeros((2, 4, 15, 256), np.float32)
    nks = np.zeros((2, 128, 1, 8, 64), np.float32)
    nvs = np.zeros((2, 128, 1, 8, 64), np.float32)
    ncs = np.zeros((2, 128, 30, 256), np.float32)
    nps = np.zeros((2, 128, 15, 256), np.float32)
    for i in range(ncores):
        b, half = i // 2, i % 2
        r = R[i]
        y_prompt[b, half * NTOK:(half + 1) * NTOK] = r["y"]
        sl = slice(i * NS, (i + 1) * NS)
        y_sample[sl, 0] = r["ys"]
        if half == 1:
            nkp[:, b] = r["kp"].reshape(2, 2048, 8, 64)
            nvp[:, b] = r["vp"].reshape(2, 2048, 8, 64)
            ncp[:, b] = r["cvp"]
            npp[:, b] = r["plp"]
        nks[:, sl, 0] = r["kso"].reshape(2, NS, 8, 64)
        nvs[:, sl, 0] = r["vso"].reshape(2, NS, 8, 64)
        ncs[:, sl] = r["cvs"]
        nps[:, sl] = r["pls"]
    return (y_prompt, y_sample, nkp, nvp, ncp, npp, nks, nvs, ncs, nps)
```

```python
import math
import numpy as np
import concourse.bass as bass
import concourse.mybir as mybir
from concourse.bass_utils import run_bass_kernel_spmd

F32, BF16 = mybir.dt.float32, mybir.dt.bfloat16
AF = mybir.ActivationFunctionType
ALU = mybir.AluOpType
AX = mybir.AxisListType

NTOK, NS = 2048, 16
NCOL = NTOK + NS
TILES = [(0, 512), (512, 512), (1024, 512), (1536, 512), (2048, 16)]
SUPER = [[0, 1], [2, 3, 4]]
SUP0 = [0, 1024]
SW = 1040
EPS = 1e-6
DILS = (1, 4, 16)
DFF = 2816
NCORES = 8
NEG = -30000.0


def _bucket(dist):
    if dist < 16:
        return dist
    v = 16 + int(np.float32(np.log(np.float32(max(dist, 1)) / np.float32(16)) / np.float32(math.log(2048 / 16)) * np.float32(16)))
    return min(v, 31)


def _bucket_np(dist):
    dist = np.asarray(dist)
    large = 16 + (np.log(np.maximum(dist, 1).astype(np.float32) / 16) / math.log(2048 / 16) * 16).astype(np.int32)
    return np.where(dist < 16, dist, np.minimum(large, 31))


class Rec:
    __slots__ = ("eng", "fn", "deps", "inc", "val", "kind", "sem", "dval")


class Prog:
    ENG = ("pe", "act", "dve", "pool", "sp")

    def __init__(self):
        self.recs = {e: [] for e in self.ENG}
        self.lastw = {}
        self.readers = {}
        self.fence_rec = None
        self.since = []

    def fence(self, fn):
        r = Rec()
        r.eng, r.fn, r.kind, r.inc, r.val, r.sem, r.dval = "dve", fn, "c", False, 0, None, 0
        deps = {}
        for e in self.ENG:
            for q in reversed(self.recs[e]):
                if q.kind == "c":
                    deps[id(q)] = q
                    break
        for q in self.since:
            if q.kind != "c":
                deps[id(q)] = q
        if self.fence_rec is not None:
            deps[id(self.fence_rec)] = self.fence_rec
        r.deps = list(deps.values())
        self.recs["dve"].append(r)
        self.fence_rec = r
        self.since = []
        self.lastw = {}
        self.readers = {}
        return r

    def op(self, eng, fn, reads=(), writes=(), kind="c"):
        r = Rec()
        r.eng, r.fn, r.kind, r.inc, r.val, r.sem, r.dval = eng, fn, kind, False, 0, None, 0
        deps = {}
        if self.fence_rec is not None:
            deps[id(self.fence_rec)] = self.fence_rec
        self.since.append(r)
        for k in reads:
            w = self.lastw.get(k)
            if w is not None:
                deps[id(w)] = w
        for k in writes:
            w = self.lastw.get(k)
            if w is not None:
                deps[id(w)] = w
            for q in self.readers.get(k, ()):
                deps[id(q)] = q
        for k in reads:
            self.readers.setdefault(k, []).append(r)
        for k in writes:
            self.lastw[k] = r
            self.readers[k] = []
        deps.pop(id(r), None)
        r.deps = list(deps.values())
        self.recs[eng].append(r)
        return r

    def dma(self, eng, fn, reads=(), writes=()):
        return self.op(eng, fn, reads, writes, kind="d")


def mm(out, lhsT, rhs, start=True, stop=True):
    return lambda e: e.matmul(out, lhsT, rhs, start=start, stop=stop)


def tr(out, in_, ident):
    return lambda e: e.transpose(out, in_, ident)


def actf(out, in_, func, bias=None, scale=None):
    kw = {}
    if bias is not None:
        kw["bias"] = bias
    if scale is not None:
        kw["scale"] = scale
    return lambda e: e.activation(out=out, in_=in_, func=func, **kw)


def tsc(out, in0, s1, s2, op0, op1=None):
    if op1 is None:
        return lambda e: e.tensor_scalar(out=out, in0=in0, scalar1=s1, scalar2=None, op0=op0)
    return lambda e: e.tensor_scalar(out=out, in0=in0, scalar1=s1, scalar2=s2, op0=op0, op1=op1)


def stt(out, in0, scalar, in1, op0, op1):
    return lambda e: e.scalar_tensor_tensor(out=out, in0=in0, scalar=scalar, in1=in1, op0=op0, op1=op1)


def tt(out, in0, in1, op):
    return lambda e: e.tensor_tensor(out=out, in0=in0, in1=in1, op=op)


def rcp(out, in_):
    return lambda e: e.reciprocal(out=out, in_=in_)


def cp(out, in_):
    return lambda e: e.tensor_copy(out=out, in_=in_)


def dm(out, in_):
    return lambda e: e.dma_start(out=out, in_=in_)


def build_program():
    import os
    KSTOP = int(os.environ.get("KSTOP", "99"))
    KSUB = int(os.environ.get("KSUB", "99"))
    KSMALL = int(os.environ.get("KSMALL", "0"))
    nc = bass.Bass("TRN2", target_bir_lowering=False)
    P = Prog()

    def din(name, shape, dt=F32):
        return nc.dram_tensor(name, list(shape), dt, kind="ExternalInput")

    def dout(name, shape, dt=F32):
        return nc.dram_tensor(name, list(shape), dt, kind="ExternalOutput")

    x_in = din("x", [NTOK, 1024]).ap()
    xs_in = din("xs", [NS, 1024]).ap()
    ck_t = din("ck", [2, NS, 2048, 512] if not KSMALL else [2, NS, 1, 512])
    cv_t = din("cv", [2, NS, 2048, 512] if not KSMALL else [2, NS, 1, 512])
    cconv_in = din("cconv", [2, NS, 30, 256]).ap()
    cpool_in = din("cpool", [2, NS, 15, 256]).ap()
    relb_in = din("relb", [32, 8]).ap()
    wgu = [din("wgu1", [2, 1024, 2 * DFF]).ap(), din("wgu2", [2, 1024, 2 * DFF]).ap()]
    wdn = [din("wdn1", [2, DFF, 1024]).ap(), din("wdn2", [2, DFF, 1024]).ap()]
    win_in = din("win", [2, 1024, 2304]).ap()
    wout_in = din("wout", [2, 1024, 1024]).ap()
    convpw_in = din("convpw", [2, 256, 256]).ap()
    poolw_in = din("poolw", [2, 4, 64, 64]).ap()
    pp_in = din("pp", [128, 232]).ap()
    cst_in = din("cst", [128, 1024]).ap()
    oh_in = din("oh", [33, 1152]).ap()
    ohs_in = din("ohs", [32, 400]).ap()
    bmask_in = din("bmask", [8, 512]).ap()
    esel_in = din("esel", [8, 256]).ap()
    convw_t = din("convw", [2, 31, 256])
    vec_t = din("vecs", [2, 4, 256])

    y_out = dout("y", [NTOK, 1024]).ap()
    ys_out = dout("ys", [NS, 1024]).ap()
    kp_out = dout("kp", [2, NTOK, 512]).ap()
    vp_out = dout("vp", [2, NTOK, 512]).ap()
    cvp_out = dout("cvp", [2, 30, 256]).ap()
    plp_out = dout("plp", [2, 15, 256]).ap()
    kso_out = dout("kso", [2, NS, 512]).ap()
    vso_out = dout("vso", [2, NS, 512]).ap()
    cvs_out = dout("cvs", [2, NS, 30, 256]).ap()
    pls_out = dout("pls", [2, NS, 15, 256]).ap()

    UQ = nc.dram_tensor("UQ", [4 * 128, NTOK], BF16).ap()
    UX_all = [[nc.dram_tensor(f"UX{l}_{i}", [4 * 128, NTOK], BF16) for i in range(3)] for l in range(2)]
    UG_all = [[nc.dram_tensor(f"UG{l}_{i}", [2 * 4 * 128, NTOK], BF16) for i in range(3)] for l in range(2)]
    cur = [0]

    def UXr(i):
        return UX_all[cur[0]][i // 4].ap()[(i % 4) * 128:(i % 4 + 1) * 128, :]

    def UGr(i, rank=0):
        return UG_all[cur[0]][i // 4].ap()[(rank * 4 + i % 4) * 128:(rank * 4 + i % 4 + 1) * 128, :]
    MIX = nc.dram_tensor("MIX", [8 * 128, NCOL], BF16).ap()
    RB_t = nc.dram_tensor("RB", [8 * 128, 1152], F32)
    RB = RB_t.ap()
    QS_t = nc.dram_tensor("QS", [NS, 512], F32)
    QS = QS_t.ap()

    off = [16640]

    def A(name, shape, dt, at=None):
        nbytes = int(np.prod(shape[1:])) * (4 if dt == F32 else 2)
        nbytes = (nbytes + 31) // 32 * 32
        if at is None:
            at = off[0]
            off[0] = at + nbytes
        return nc.alloc_sbuf_tensor_at(name, list(shape), dt, offset=at), at + nbytes

    hT, _ = A("hT", [128, 8, NCOL], F32)
    EB, _ = A("EB", [128, 24, 256], BF16)
    ident_f, _ = A("ident_f", [128, 128], F32)
    ones_f, _ = A("ones_f", [128, 128], F32)
    ident_b, _ = A("ident_b", [128, 128], BF16)
    ones_b, _ = A("ones_b", [128, 128], BF16)
    bd_b, _ = A("bd_b", [128, 128], BF16)
    on0_b, _ = A("on0_b", [128, 128], BF16)
    on1_b, _ = A("on1_b", [128, 128], BF16)
    pp, _ = A("pp", [128, 232], F32)
    cstf, _ = A("cstf", [128, 640], F32)
    SBt, _ = A("SBt", [128, 24], F32)
    rb0, _ = A("rb0", [16, 8], F32)
    bmask, _ = A("bmask", [8, 512], F32)
    esel, _ = A("esel", [8, 256], F32)
    dummy, _ = A("dummy", [128, 8], F32)
    csT, _ = A("csT", [128, 2, NS, 31], F32)
    cpT, _ = A("cpT", [128, 2, NS, 16], F32)
    glus, _ = A("glus", [NS, 256], F32)
    cs16, _ = A("cs16", [NS, 256], F32)
    S0 = off[0]
    assert S0 <= 116 * 1024, S0
    print('S0', S0)

    def FENCE():
        P.fence(lambda e: e.memset(dummy[:], 0.0))

    def gv(l, k, c):
        return pp[:, (l * 3 + k) * 8 + c:(l * 3 + k) * 8 + c + 1]
    GQ = lambda l: pp[:, 48 + l:49 + l]
    GK8 = lambda l: pp[:, 50 + l:51 + l]
    CW = lambda l, cc, w: pp[:, 52 + (l * 2 + cc) * 31 + w:53 + (l * 2 + cc) * 31 + w]
    VEC = lambda l, k, cc: pp[:, 176 + (l * 4 + k) * 2 + cc:177 + (l * 4 + k) * 2 + cc]
    HM = pp[:, 192:193]
    HB = pp[:, 193:194]
    INVW = lambda cc: pp[:, 194 + cc:195 + cc]
    ICNT = lambda cc: pp[:, 196 + cc * 16:196 + cc * 16 + 16]
    ZB = pp[:, 230:231]
    EPSC = pp[:, 231:232]

    ps = [nc.alloc_psum_tensor(f"ps{i}", [128, 512], F32) for i in range(7)]
    ps.append(nc.alloc_psum_tensor("ps7", [128, 512], F32))

    CK = "const"

    P.dma("sp", dm(pp[:], pp_in), writes=[CK])
    P.dma("sp", dm(cstf[:], cst_in[:, 0:640]), writes=["cstf"])
    P.dma("sp", dm(bmask[:], bmask_in), writes=[CK])
    P.dma("sp", dm(esel[:], esel_in), writes=[CK])
    P.op("dve", cp(ident_f[:], cstf[:, 0:128]), reads=["cstf"], writes=[CK])
    P.op("dve", cp(ones_f[:], cstf[:, 128:256]), reads=["cstf"], writes=[CK])
    P.op("dve", cp(ident_b[:], cstf[:, 0:128]), reads=["cstf"], writes=[CK])
    P.op("dve", cp(ones_b[:], cstf[:, 128:256]), reads=["cstf"], writes=[CK])
    P.op("dve", cp(bd_b[:], cstf[:, 256:384]), reads=["cstf"], writes=[CK])
    P.op("dve", cp(on0_b[:], cstf[:, 384:512]), reads=["cstf"], writes=[CK])
    P.op("dve", cp(on1_b[:], cstf[:, 512:640]), reads=["cstf"], writes=[CK])
    P.op("dve", tsc(pp[:, 48:50], pp[:, 48:50], 0.125, None, ALU.mult), reads=[CK], writes=[CK])

    sc = [S0]

    def SA(name, shape, dt):
        t, e = A(name, shape, dt, at=sc[0])
        sc[0] = e
        return t
    rbx = SA("rbx", [33, 8], F32)
    ohp = SA("ohp", [33, 1152], F32)
    ohs = SA("ohs", [32, 400], F32)
    lh = SA("lh", [33, 128], F32)
    rsb = SA("rsb", [128, 1152], F32)
    tbs = SA("tbs", [128, 3, 256], F32)
    P.op("dve", lambda e: e.memset(rbx[:], NEG), writes=["rbx"])
    P.dma("sp", dm(rbx[0:32, :], relb_in), reads=[], writes=["rbx"])
    P.dma("sp", dm(ohp[:], oh_in), writes=["ohp"])
    P.dma("sp", dm(ohs[:], ohs_in), writes=["ohs"])
    for d in range(3):
        P.op("pe", mm(ps[0][:, d * 8:(d + 1) * 8], ohs[:, d * 128:(d + 1) * 128], rbx[0:32, :]),
             reads=["ohs", "rbx"], writes=[("ps", 0)])
    P.op("dve", cp(SBt[:], ps[0][:, 0:24]), reads=[("ps", 0)], writes=[CK])
    P.op("pe", mm(ps[1][0:16, 0:8], ohs[:, 384:400], rbx[0:32, :]), reads=["ohs", "rbx"], writes=[("ps", 1)])
    P.op("dve", cp(rb0[:], ps[1][0:16, 0:8]), reads=[("ps", 1)], writes=[CK])
    for h in range(8):
        P.op("dve", tsc(lh[:], ones_f[0:33, :], rbx[:, h:h + 1], None, ALU.mult), reads=[CK, "rbx"], writes=["lh"])
        for d in range(3):
            P.op("pe", mm(ps[2 + d][:, 0:384], lh[:], ohp[:, d * 384:(d + 1) * 384]),
                 reads=["lh", "ohp"], writes=[("ps", 2 + d)])
            P.op("act", actf(rsb[:, d * 384:(d + 1) * 384], ps[2 + d][:, 0:384], AF.Copy),
                 reads=[("ps", 2 + d)], writes=["rsb"])
        P.dma("sp", dm(RB[h * 128:(h + 1) * 128, :], rsb[:]), reads=["rsb"], writes=[("RB", h)])
        for d in range(3):
            src = bass.AP(tensor=RB_t, offset=h * 128 * 1152 + d * 384 + 127, ap=[[1151, 128], [1, 256]])
            P.dma("sp", dm(tbs[:, d, :], src), reads=[("RB", h)], writes=["tbs"])
        for d in range(3):
            P.op("act", actf(EB[:, d * 8 + h, :], tbs[:, d, :], AF.Exp), reads=["tbs"], writes=[CK])

    sc[0] = S0
    xtok = [SA(f"xtok{i}", [128, 1024], F32) for i in range(2)]
    for blk in range(17):
        xb = xtok[blk % 2]
        n = 128 if blk < 16 else NS
        srcx = x_in[blk * 128:(blk + 1) * 128, :] if blk < 16 else xs_in
        P.dma("sp", dm(xb[0:n, :], srcx), writes=[("xtok", blk % 2)])
        for dc in range(8):
            bank = dc % 4
            P.op("pe", tr(ps[bank][:, 0:n], xb[0:n, dc * 128:(dc + 1) * 128], ident_f[0:n, 0:n]),
                 reads=[("xtok", blk % 2), CK], writes=[("ps", bank)])
            eng = "act" if dc % 2 == 0 else "dve"
            fn = actf(hT[:, dc, blk * 128:blk * 128 + n], ps[bank][:, 0:n], AF.Copy) if eng == "act" else \
                cp(hT[:, dc, blk * 128:blk * 128 + n], ps[bank][:, 0:n])
            ti = min(blk // 4, 4)
            P.op(eng, fn, reads=[("ps", bank)], writes=[("hT", dc, ti)])

    def tile_cols(ti):
        return TILES[ti]

    def rmsnorm(l, gi, s, xn, sq, rstd):
        cs0 = SUP0[s]
        for ti in SUPER[s]:
            c0, n = TILES[ti]
            for dc in range(8):
                sb = sq[dc % 2]
                P.op("act", actf(sb[:, 0:n], hT[:, dc, c0:c0 + n], AF.Square),
                     reads=[("hT", dc, ti)], writes=[("sq", dc % 2)])
                P.op("pe", mm(ps[6][:, 0:n], ones_b[:], sb[:, 0:n], start=(dc == 0), stop=(dc == 7)),
                     reads=[("sq", dc % 2), CK], writes=[("ps", 6)])
            KVAR = os.environ.get("KVAR", "")
            if "A" in KVAR:
                P.op("dve", cp(rstd[:, 0:n], ps[6][:, 0:n]), reads=[("ps", 6), CK], writes=["rstd"])
            else:
                P.op("act", actf(rstd[:, 0:n], ps[6][:, 0:n], AF.Sqrt, bias=EPSC, scale=1.0 / 1024),
                     reads=[("ps", 6), CK], writes=["rstd"])
            if "R" not in KVAR:
                P.op("dve", rcp(rstd[:, 0:n], rstd[:, 0:n]), reads=["rstd"], writes=["rstd"])
            for dc in range(8):
                if "B" in KVAR:
                    break
                P.op("dve", stt(xn[:, dc, c0 - cs0:c0 - cs0 + n], hT[:, dc, c0:c0 + n], gv(l, gi, dc),
                                rstd[:, 0:n], ALU.mult, ALU.mult),
                     reads=[("hT", dc, ti), "rstd", CK], writes=[("xn", dc, ti)])

    def ffn(l, which):
        gi = 0 if which == 0 else 2
        sc[0] = S0
        xn = SA("xn_f", [128, 8, SW], BF16)
        actb = SA("actb", [128, 22, SW], BF16)
        wg = [SA(f"wg{i}", [128, 8, 128], BF16) for i in range(3)]
        wu = [SA(f"wu{i}", [128, 8, 128], BF16) for i in range(3)]
        wd = [SA(f"wd{i}", [128, 22, 128], BF16) for i in range(2)]
        sq = [SA(f"sq{i}", [128, 512], BF16) for i in range(2)]
        rstd = SA("rstd", [128, 512], F32)
        sg = [SA(f"sg{i}", [128, 512], F32) for i in range(2)]
        assert sc[0] <= 229376, sc[0]
        W = wgu[which][l].rearrange("(kc p) f -> p kc f", p=128)
        Wd = wdn[which][l].rearrange("(kc p) f -> p kc f", p=128)
        cnt = 0
        for s in range(2):
            cs0 = SUP0[s]
            rmsnorm(l, gi, s, xn, sq, rstd)
            if KSUB <= 2:
                return
            for j in range(22):
                if KSUB <= 3 and j >= 1:
                    break
                b = j % 3
                P.dma("pool", dm(wg[b][:], W[:, :, j * 128:(j + 1) * 128]), writes=[("wg", b)])
                P.dma("pool", dm(wu[b][:], W[:, :, DFF + j * 128:DFF + (j + 1) * 128]), writes=[("wu", b)])
                for ti in SUPER[s]:
                    c0, n = TILES[ti]
                    lc = c0 - cs0
                    bg, bu = cnt % 2, 2 + cnt % 2
                    for kc in range(8):
                        P.op("pe", mm(ps[bg][:, 0:n], wg[b][:, kc, :], xn[:, kc, lc:lc + n], start=(kc == 0), stop=(kc == 7)),
                             reads=[("wg", b), ("xn", kc, ti)], writes=[("ps", bg)])
                    for kc in range(8):
                        P.op("pe", mm(ps[bu][:, 0:n], wu[b][:, kc, :], xn[:, kc, lc:lc + n], start=(kc == 0), stop=(kc == 7)),
                             reads=[("wu", b), ("xn", kc, ti)], writes=[("ps", bu)])
                    P.op("act", actf(sg[cnt % 2][:, 0:n], ps[bg][:, 0:n], AF.Silu), reads=[("ps", bg)], writes=[("sg", cnt % 2)])
                    P.op("dve", tt(actb[:, j, lc:lc + n], sg[cnt % 2][:, 0:n], ps[bu][:, 0:n], ALU.mult),
                         reads=[("sg", cnt % 2), ("ps", bu)], writes=[("actb", j, ti)])
                    cnt += 1
            if KSUB <= 4:
                return
            for dc in range(8):
                b = dc % 2
                P.dma("pool", dm(wd[b][:], Wd[:, :, dc * 128:(dc + 1) * 128]), writes=[("wd", b)])
                for ti in SUPER[s]:
                    c0, n = TILES[ti]
                    lc = c0 - cs0
                    bk = 4 + cnt % 2
                    for j in range(22):
                        P.op("pe", mm(ps[bk][:, 0:n], wd[b][:, j, :], actb[:, j, lc:lc + n], start=(j == 0), stop=(j == 21)),
                             reads=[("wd", b), ("actb", j, ti)], writes=[("ps", bk)])
                    P.op("dve", stt(hT[:, dc, c0:c0 + n], ps[bk][:, 0:n], 0.5, hT[:, dc, c0:c0 + n], ALU.mult, ALU.add),
                         reads=[("ps", bk), ("hT", dc, ti)], writes=[("hT", dc, ti)])
                    cnt += 1
            if KSUB <= 5:
                return

    cc_sems = []

    def mixer(l):
        sc[0] = S0
        cur[0] = l
        UX_ts, UG_ts = UX_all[l], UG_all[l]
        xn = SA("xn_m", [128, 8, SW], BF16)
        wi = [SA(f"wi{i}", [128, 8, 128], BF16) for i in range(4)]
        sq = [SA(f"sqm{i}", [128, 512], BF16) for i in range(2)]
        rstd = SA("rstdm", [128, 512], F32)
        qf = [SA(f"qf{i}", [128, 512], F32) for i in range(4)]
        sqn = [SA(f"sqn{i}", [128, 512], BF16) for i in range(2)]
        rs3 = [SA(f"rsq{i}", [128, 512], F32) for i in range(3)]
        ob = [SA(f"ob{i}", [128, 512], BF16) for i in range(3)]
        ko = [SA(f"ko{i}", [128, 512], F32) for i in range(2)]
        us_q = SA("us_q", [NS, 512], F32)
        us_k = SA("us_k", [NS, 512], F32)
        us_v = SA("us_v", [NS, 512], F32)
        tail = SA("tail", [32, 128], F32)
        stg = [SA(f"stg{i}", [120, 256], F32) for i in range(2)]
        P1_END = sc[0]
        Wi = win_in[l].rearrange("(kc p) f -> p kc f", p=128)
        P.dma("sp", dm(cvs_out[l][:, 0:29, :], cconv_in[l][:, 1:30, :]), writes=[])
        P.dma("sp", dm(pls_out[l][:, 0:14, :], cpool_in[l][:, 1:15, :]), writes=[])
        for t in range(4):
            sb_ = stg[t % 2]
            P.dma("sp", dm(sb_[:], cconv_in[l][4 * t:4 * t + 4].rearrange("b w c -> (b w) c")), writes=[("stg", t % 2)])
            for cc in range(2):
                P.op("pe", tr(ps[6][:, 0:120], sb_[:, cc * 128:(cc + 1) * 128], ident_f[0:120, 0:120]), reads=[("stg", t % 2), CK], writes=[("ps", 6)])
                P.op("dve", cp(csT[:, cc, 4 * t:4 * t + 4, 0:30], ps[6][:, 0:120].rearrange("p (b w) -> p b w", w=30)),
                     reads=[("ps", 6)], writes=["csT"])
        for t in range(2):
            sb_ = stg[t % 2]
            P.dma("sp", dm(sb_[:], cpool_in[l][8 * t:8 * t + 8].rearrange("b w c -> (b w) c")), writes=[("stg", t % 2)])
            for cc in range(2):
                P.op("pe", tr(ps[6][:, 0:120], sb_[:, cc * 128:(cc + 1) * 128], ident_f[0:120, 0:120]), reads=[("stg", t % 2), CK], writes=[("ps", 6)])
                P.op("dve", cp(cpT[:, cc, 8 * t:8 * t + 8, 0:15], ps[6][:, 0:120].rearrange("p (b w) -> p b w", w=15)),
                     reads=[("ps", 6)], writes=["cpT"])
        cnt = [0]
        kcnt = [0]

        def wload(oc):
            b = cnt[0] % 4
            cnt[0] += 1
            P.dma("pool", dm(wi[b][:], Wi[:, :, oc * 128:(oc + 1) * 128]), writes=[("wi", b)])
            return b

        def proj(b, ti, bank, cs0):
            c0, n = TILES[ti]
            lc = c0 - cs0
            for kc in range(8):
                P.op("pe", mm(ps[bank][:, 0:n], wi[b][:, kc, :], xn[:, kc, lc:lc + n], start=(kc == 0), stop=(kc == 7)),
                     reads=[("wi", b), ("xn", kc, ti)], writes=[("ps", bank)])

        def transpose_out(src, skey, ti, dst_tok, dst_cols, sample_dst, sample_key):
            if ti < 4:
                if dst_tok is None:
                    return
                kb = kcnt[0] % 2
                kcnt[0] += 1
                tb = 6 + kb
                for bq in range(4):
                    P.op("pe", tr(ps[tb][:, bq * 128:(bq + 1) * 128], src[:, bq * 128:(bq + 1) * 128], ident_f[:]),
                         reads=[skey, CK], writes=[("ps", tb)])
                P.op("act", actf(ko[kb][:], ps[tb][:], AF.Copy), reads=[("ps", tb)], writes=[("ko", kb)])
                dst = dst_tok[ti * 512:(ti + 1) * 512, dst_cols[0]:dst_cols[1]].rearrange("(b p) f -> p b f", p=128)
                P.dma("sp", dm(dst, ko[kb][:].rearrange("p (b f) -> p b f", b=4)), reads=[("ko", kb)], writes=[])
            else:
                P.op("pe", tr(ps[6][0:NS, 0:128], src[:, 0:NS], ident_f[:]), reads=[skey, CK], writes=[("ps", 6)])
                P.op("dve", cp(sample_dst, ps[6][0:NS, 0:128]), reads=[("ps", 6)], writes=[sample_key])

        for s in range(2):
            cs0 = SUP0[s]
            rmsnorm(l, 1, s, xn, sq, rstd)
            pend1 = []
            qcnt = [0]

            def qkv_post(oc, ti, bank, i3, i2):
                c0, n = TILES[ti]
                fb, obb = qf[i3], ob[i3 % 3]
                fkey, okey = ("qf", i3), ("ob", i3 % 3)
                rsb = rs3[i3 % 3]
                rkey = ("rsq", i3 % 3)
                P.op("act", actf(fb[:, 0:n], ps[bank][:, 0:n], AF.Copy), reads=[("ps", bank)], writes=[fkey])
                if oc < 8:
                    sb = sqn[i2]
                    P.op("act", actf(sb[:, 0:n], ps[bank][:, 0:n], AF.Square), reads=[("ps", bank)], writes=[("sqn", i2)])
                    P.op("pe", mm(ps[4 + i2][:, 0:n], bd_b[:], sb[:, 0:n]), reads=[("sqn", i2), CK], writes=[("ps", 4 + i2)])
                    P.op("act", actf(rsb[:, 0:n], ps[4 + i2][:, 0:n], AF.Sqrt, bias=EPSC, scale=1.0 / 64),
                         reads=[("ps", 4 + i2), CK], writes=[rkey])
                    P.op("dve", rcp(rsb[:, 0:n], rsb[:, 0:n]), reads=[rkey], writes=[rkey])
                    g = GQ(l) if oc < 4 else GK8(l)
                    P.op("dve", stt(fb[:, 0:n], fb[:, 0:n], g, rsb[:, 0:n], ALU.mult, ALU.mult),
                         reads=[fkey, rkey, CK], writes=[fkey])
                if ti < 4:
                    P.op("act", actf(obb[:, 0:n], fb[:, 0:n], AF.Copy), reads=[fkey], writes=[okey])
                    if oc < 4:
                        P.dma("sp", dm(UQ[oc * 128:(oc + 1) * 128, c0:c0 + n], obb[:, 0:n]), reads=[okey], writes=[("UQ", oc)])
                    else:
                        P.dma("sp", dm(UXr(oc - 4)[:, c0:c0 + n], obb[:, 0:n]), reads=[okey], writes=[("UX", oc - 4)])

            def qkv_post_b(oc, ti, bank, i3, i2):
                fb = qf[i3]
                fkey = ("qf", i3)
                if oc < 4:
                    transpose_out(fb, fkey, ti, None, None, us_q[:, oc * 128:(oc + 1) * 128], "us_q")
                elif oc < 8:
                    transpose_out(fb, fkey, ti, kp_out[l], ((oc - 4) * 128, (oc - 3) * 128), us_k[:, (oc - 4) * 128:(oc - 3) * 128], "us_k")
                else:
                    transpose_out(fb, fkey, ti, vp_out[l], ((oc - 8) * 128, (oc - 7) * 128), us_v[:, (oc - 8) * 128:(oc - 7) * 128], "us_v")

            pend2 = []

            def step_post(flush=False):
                if pend1 and (flush or len(pend1) > 1):
                    a = pend1.pop(0)
                    qkv_post(*a)
                    pend2.append(a)
                if pend2 and (flush or len(pend2) > 1):
                    qkv_post_b(*pend2.pop(0))

            for oc in range(12):
                b = wload(oc)
                for ti in SUPER[s]:
                    bank = cnt[0] % 4
                    cnt[0] += 1
                    proj(b, ti, bank, cs0)
                    qcnt[0] += 1
                    pend1.append((oc, ti, bank, qcnt[0] % 4, qcnt[0] % 2))
                    step_post()
            while pend1 or pend2:
                step_post(flush=True)
            for cc in range(2):
                bv = wload(12 + cc)
                bg = wload(14 + cc)
                for ti in SUPER[s]:
                    c0, n = TILES[ti]
                    bank1 = cnt[0] % 4
                    bank2 = (cnt[0] + 1) % 4
                    cnt[0] += 2
                    proj(bv, ti, bank1, cs0)
                    proj(bg, ti, bank2, cs0)
                    i3 = cnt[0] % 3
                    fb, obb = qf[i3], ob[i3]
                    fkey, okey = ("qf", i3), ("ob", i3)
                    P.op("act", actf(fb[:, 0:n], ps[bank2][:, 0:n], AF.Sigmoid), reads=[("ps", bank2)], writes=[fkey])
                    P.op("dve", tt(fb[:, 0:n], fb[:, 0:n], ps[bank1][:, 0:n], ALU.mult), reads=[fkey, ("ps", bank1)], writes=[fkey])
                    if ti < 4:
                        P.op("act", actf(obb[:, 0:n], fb[:, 0:n], AF.Copy), reads=[fkey], writes=[okey])
                        P.dma("sp", dm(UXr(8 + cc)[:, c0:c0 + n], obb[:, 0:n]), reads=[okey], writes=[("UX", 8 + cc)])
                    if ti == 3:
                        P.op("pe", tr(ps[6][0:32, 0:128], fb[:, 480:512], ident_f[:]), reads=[fkey, CK], writes=[("ps", 6)])
                        P.op("dve", cp(tail[:], ps[6][0:32, 0:128]), reads=[("ps", 6)], writes=["tail"])
                        P.dma("sp", dm(cvp_out[l][:, cc * 128:(cc + 1) * 128], tail[2:32, :]), reads=["tail"], writes=[])
                    if ti == 4:
                        P.op("dve", cp(csT[:, cc, :, 30], fb[:, 0:NS]), reads=[fkey], writes=["csT"])
                        P.op("pe", tr(ps[6][0:NS, 0:128], fb[:, 0:NS], ident_f[:]), reads=[fkey, CK], writes=[("ps", 6)])
                        P.op("dve", cp(glus[:, cc * 128:(cc + 1) * 128], ps[6][0:NS, 0:128]), reads=[("ps", 6)], writes=["glus"])
            for cc in range(2):
                b = wload(16 + cc)
                for ti in SUPER[s]:
                    c0, n = TILES[ti]
                    bank = cnt[0] % 4
                    cnt[0] += 1
                    proj(b, ti, bank, cs0)
                    i3 = cnt[0] % 3
                    fb, obb = qf[i3], ob[i3]
                    fkey, okey = ("qf", i3), ("ob", i3)
                    P.op("act", actf(fb[:, 0:n], ps[bank][:, 0:n], AF.Copy), reads=[("ps", bank)], writes=[fkey])
                    if ti < 4:
                        P.op("act", actf(obb[:, 0:n], fb[:, 0:n], AF.Copy), reads=[fkey], writes=[okey])
                        P.dma("sp", dm(UXr(10 + cc)[:, c0:c0 + n], obb[:, 0:n]), reads=[okey], writes=[("UX", 10 + cc)])
                    if ti == 3:
                        P.op("pe", tr(ps[6][0:16, 0:128], fb[:, 496:512], ident_f[:]), reads=[fkey, CK], writes=[("ps", 6)])
                        P.op("dve", cp(tail[0:16, :], ps[6][0:16, 0:128]), reads=[("ps", 6)], writes=["tail"])
                        P.dma("sp", dm(plp_out[l][:, cc * 128:(cc + 1) * 128], tail[1:16, :]), reads=["tail"], writes=[])
                    if ti == 4:
                        P.op("dve", cp(cpT[:, cc, :, 15], fb[:, 0:NS]), reads=[fkey], writes=["cpT"])
                        P.op("pe", tr(ps[6][0:NS, 0:128], fb[:, 0:NS], ident_f[:]), reads=[fkey, CK], writes=[("ps", 6)])
                        P.op("dve", cp(cs16[:, cc * 128:(cc + 1) * 128], ps[6][0:NS, 0:128]), reads=[("ps", 6)], writes=["cs16"])

        for g3 in range(3):
            sem = nc.alloc_semaphore(f"ccsem{l}_{g3}")
            cc_sems.append(sem)
            r = P.op("pool", lambda e, g3=g3: e.collective_compute("AllGather", ALU.bypass, replica_groups=[[2 * i, 2 * i + 1] for i in range(NCORES // 2)],
                                                                    ins=[UX_ts[g3].ap().opt()], outs=[UG_ts[g3].ap().opt()]),
                     reads=[("UX", i) for i in range(g3 * 4, g3 * 4 + 4)], writes=[("UG", g3)], kind="cc")
            r.sem = sem

        P.dma("sp", dm(kso_out[l], us_k[:]), reads=["us_k"], writes=[])
        P.dma("sp", dm(vso_out[l], us_v[:]), reads=["us_v"], writes=[])
        P.dma("sp", dm(cvs_out[l][:, 29, :], glus[:]), reads=["glus"], writes=[])
        P.dma("sp", dm(pls_out[l][:, 14, :], cs16[:]), reads=["cs16"], writes=[])
        P.dma("sp", dm(QS, us_q[:]), reads=["us_q"], writes=["QS"])

        if l == 0 and KSTOP < 3:
            return
        KP3 = int(os.environ.get("KP3", "99"))
        sc[0] = P1_END
        Kg = [SA(f"Kg{i}", [128, 3, 512], F32) for i in range(2)]
        Vg = [SA(f"Vg{i}", [128, 3, 512], F32) for i in range(2)]
        qb = [SA(f"qb{i}", [128, 512], F32) for i in range(2)]
        tmp2 = [SA(f"stmp{i}", [128, 3, 512], F32) for i in range(2)]
        lg2 = [SA(f"lg{i}", [128, 24], F32) for i in range(2)]
        pexp2 = [SA(f"pexp{i}", [128, 24], F32) for i in range(2)]
        Rr2 = [SA(f"Rr{i}", [8, 520], F32) for i in range(2)]
        t16 = SA("t16", [NS, 512], F32)
        l0 = SA("l0", [NS, 8], F32)
        p0 = SA("p0", [NS, 8], F32)
        dn = SA("dn", [NS, 8], F32)
        numt = SA("numt", [NS, 512], F32)
        msT = SA("msT", [128, 4, NS], BF16)
        assert sc[0] <= 229376, sc[0]
        PS_N, PS_D, PS_AN, PS_AD = 0, 1, 2, 3
        for b in range(NS if not KSMALL else 0):
            i2 = b % 2
            tmp, lg, pexp, Rr = tmp2[i2], lg2[i2], pexp2[i2], Rr2[i2]
            ktmp, klg, kpx, krr = ("stmp", i2), ("lg", i2), ("pexp", i2), ("Rr", i2)
            PS_N, PS_D = (0, 1) if i2 == 0 else (4, 5)
            P.dma("sp", dm(qb[i2][:], bass.AP(tensor=QS_t, offset=b * 512, ap=[[0, 128], [1, 512]])), reads=["QS"], writes=[("qb", i2)])
            for d, dil in enumerate(DILS):
                o = ((l * NS + b) * 2048 + (2048 - 128 * dil)) * 512
                P.dma("sp", dm(Kg[i2][:, d, :], bass.AP(tensor=ck_t, offset=o, ap=[[dil * 512, 128], [1, 512]])), writes=[("Kg", i2, d)])
                P.dma("sp", dm(Vg[i2][:, d, :], bass.AP(tensor=cv_t, offset=o, ap=[[dil * 512, 128], [1, 512]])), writes=[("Vg", i2, d)])
            for d in range(3):
                P.op("dve", tt(tmp[:, d, :], Kg[i2][:, d, :], qb[i2][:], ALU.mult), reads=[("Kg", i2, d), ("qb", i2)], writes=[ktmp])
            P.op("dve", lambda e, lg=lg, tmp=tmp: e.tensor_reduce(out=lg[:], in_=tmp[:].rearrange("p a (h d) -> p (a h) d", d=64), axis=AX.X, op=ALU.add),
                 reads=[ktmp], writes=[klg])
            P.op("dve", tt(lg[:], lg[:], SBt[:], ALU.add), reads=[klg, CK], writes=[klg])
            P.op("act", actf(pexp[:], lg[:], AF.Exp), reads=[klg], writes=[kpx])
            for d in range(3):
                P.op("pe", mm(ps[PS_N][0:8, :], pexp[:, d * 8:(d + 1) * 8], Vg[i2][:, d, :], start=(d == 0), stop=(d == 2)),
                     reads=[kpx, ("Vg", i2, d)], writes=[("ps", PS_N)])
            for d in range(3):
                P.op("pe", mm(ps[PS_D][0:8, 0:8], pexp[:, d * 8:(d + 1) * 8], ones_f[:, 0:8], start=(d == 0), stop=(d == 2)),
                     reads=[kpx, CK], writes=[("ps", PS_D)])
            P.op("dve", tt(Rr[:, 0:512], ps[PS_N][0:8, :], bmask[:], ALU.mult), reads=[("ps", PS_N), CK], writes=[krr])
            P.op("dve", tt(Rr[:, 512:520], ps[PS_D][0:8, 0:8], ident_f[0:8, 0:8], ALU.mult), reads=[("ps", PS_D), CK], writes=[krr])
            P.op("pe", mm(ps[PS_AN][0:NS, :], esel[:, b * 16:(b + 1) * 16], Rr[:, 0:512], start=(b == 0), stop=(b == NS - 1)),
                 reads=[krr, CK], writes=[("ps", PS_AN)])
            P.op("pe", mm(ps[PS_AD][0:NS, 0:8], esel[:, b * 16:(b + 1) * 16], Rr[:, 512:520], start=(b == 0), stop=(b == NS - 1)),
                 reads=[krr, CK], writes=[("ps", PS_AD)])
        P.op("dve", tt(t16[:], us_q[:], us_k[:], ALU.mult), reads=["us_q", "us_k"], writes=["t16"])
        P.op("dve", lambda e: e.tensor_reduce(out=l0[:], in_=t16[:].rearrange("p (h d) -> p h d", d=64), axis=AX.X, op=ALU.add),
             reads=["t16"], writes=["l0"])
        P.op("dve", tt(l0[:], l0[:], rb0[:], ALU.add), reads=["l0", CK], writes=["l0"])
        P.op("act", actf(p0[:], l0[:], AF.Exp), reads=["l0"], writes=["p0"])
        P.op("dve", tsc(p0[:], p0[:], 3.0, None, ALU.mult), reads=["p0"], writes=["p0"])
        P.op("dve", tt(dn[:], ps[PS_AD][0:NS, 0:8], p0[:], ALU.add), reads=[("ps", PS_AD), "p0"], writes=["dn"])
        P.op("dve", lambda e: e.reciprocal(out=dn[:], in_=dn[:]), reads=["dn"], writes=["dn"])
        for h in range(8):
            P.op("dve", stt(numt[:, h * 64:(h + 1) * 64], us_v[:, h * 64:(h + 1) * 64], p0[:, h:h + 1],
                            ps[PS_AN][0:NS, h * 64:(h + 1) * 64], ALU.mult, ALU.add),
                 reads=["us_v", "p0", ("ps", PS_AN)], writes=[("numt", h)])
        for h in range(8):
            P.op("dve", tsc(numt[:, h * 64:(h + 1) * 64], numt[:, h * 64:(h + 1) * 64], dn[:, h:h + 1], None, ALU.mult),
                 reads=[("numt", h), "dn"], writes=[("numt", h)])
        for c in range(4):
            P.op("pe", tr(ps[4][:, 0:NS], numt[:, c * 128:(c + 1) * 128], ident_f[0:NS, 0:NS]),
                 reads=[("numt", 2 * c), ("numt", 2 * c + 1), CK], writes=[("ps", 4)])
            P.op("dve", cp(msT[:, c, :], ps[4][:, 0:NS]), reads=[("ps", 4)], writes=["msT"])
        for c in range(4):
            P.dma("sp", dm(MIX[c * 128:(c + 1) * 128, NTOK:NCOL], msT[:, c, :]), reads=["msT"], writes=[("MIXs", c)])
        FENCE()
        if l == 0 and KSTOP < 4:
            return

        sc[0] = S0
        qn = SA("qn", [128, NTOK], BF16)
        kst = SA("kst", [128, 2 * NTOK], BF16)
        vst = SA("vst", [128, 2 * NTOK], BF16)
        q4 = SA("q4", [128, 4, 512], BF16)
        q16 = SA("q16", [128, 16, 128], BF16)
        k4 = SA("k4", [128, 4, 640], BF16)
        k16 = SA("k16", [128, 16, 256], BF16)
        varr = SA("varr", [128, 4096], BF16)
        Vp0 = SA("Vp0", [128, 32, 128], BF16)
        Vp1 = SA("Vp1", [128, 32, 128], BF16)
        et = [SA(f"et{i}", [128, 256], BF16) for i in range(6)]
        pt = [SA(f"pt{i}", [128, 256], BF16) for i in range(6)]
        accn = SA("accn", [128, NTOK], F32)
        accd = SA("accd", [128, NTOK], F32)
        ao = SA("ao", [128, NTOK], BF16)
        assert sc[0] <= 229376, sc[0]
        P.op("pool", lambda e: e.memset(Vp0[:], 0.0), reads=[], writes=["Vp0"])
        P.op("pool", lambda e: e.memset(Vp1[:], 0.0), reads=[], writes=["Vp1"])
        scnt = [0]
        gcnt = [0]
        for c in range(4):
            P.dma("sp", dm(qn[:], UQ[c * 128:(c + 1) * 128, :]), writes=["qn"])
            P.dma("sp", dm(kst[:, 0:NTOK], UGr(c)), writes=["kst"])
            P.dma("sp", dm(kst[:, NTOK:], UXr(c)), writes=["kst"])
            P.dma("sp", dm(vst[:, 0:NTOK], UGr(4 + c)), writes=["vst"])
            P.dma("sp", dm(vst[:, NTOK:], UXr(4 + c)), writes=["vst"])
            P.op("pool", cp(q4[:], qn[:].rearrange("p (m r) -> p r m", r=4)), reads=["qn"], writes=["q4"])
            P.op("pool", cp(q16[:], qn[:].rearrange("p (m r) -> p r m", r=16)), reads=["qn"], writes=["q16"])
            P.op("dve", cp(k4[:], kst[:, 1536:4096].rearrange("p (m r) -> p r m", r=4)), reads=["kst"], writes=["k4"])
            P.op("dve", cp(k16[:], kst[:].rearrange("p (m r) -> p r m", r=16)), reads=["kst"], writes=["k16"])
            for di, dil in enumerate(DILS):
                nres = dil
                nqb = 16 // dil
                nkb = nqb + 1
                if dil == 1:
                    karr = lambda rho, kb: kst[:, 1920 + kb * 128:1920 + (kb + 1) * 128]
                    qarr = lambda rho, lo, hi: qn[:, lo:hi]
                    vsrc = lambda rho, kb: vst[:, 1920 + kb * 128:1920 + (kb + 1) * 128]
                    vkey, kkey, qkey = "vst", "kst", "qn"
                elif dil == 4:
                    karr = lambda rho, kb: k4[:, rho, kb * 128:(kb + 1) * 128]
                    qarr = lambda rho, lo, hi: q4[:, rho, lo:hi]
                    P.op("pool", cp(varr[:, 0:2560].rearrange("p (r m) -> p r m", r=4), vst[:, 1536:4096].rearrange("p (m r) -> p r m", r=4)),
                         reads=["vst"], writes=["varr"])
                    vsrc = lambda rho, kb: varr[:, rho * 640 + kb * 128:rho * 640 + (kb + 1) * 128]
                    vkey, kkey, qkey = "varr", "k4", "q4"
                else:
                    karr = lambda rho, kb: k16[:, rho, kb * 128:(kb + 1) * 128]
                    qarr = lambda rho, lo, hi: q16[:, rho, lo:hi]
                    P.op("pool", cp(varr[:].rearrange("p (r m) -> p r m", r=16), vst[:].rearrange("p (m r) -> p r m", r=16)),
                         reads=["vst"], writes=["varr"])
                    vsrc = lambda rho, kb: varr[:, rho * 256 + kb * 128:rho * 256 + (kb + 1) * 128]
                    vkey, kkey, qkey = "varr", "k16", "q16"
                blks = [(rho, kb) for rho in range(nres) for kb in range(nkb)]
                if KP3 <= 1:
                    continue
                KDIL = os.environ.get("KDIL", "")
                if KDIL and str(di) not in KDIL:
                    continue
                for g0 in range(0, len(blks), 4):
                    grp = blks[g0:g0 + 4]
                    half = (g0 // 4) % 2
                    if os.environ.get("KNOBC", "0") == "1":
                        half = 0
                    pbank = ps[7 - half][:].bitcast(BF16)[:, 0:512]
                    for i, (rho, kb) in enumerate(grp):
                        P.op("pe", tr(pbank[:, i * 128:(i + 1) * 128], vsrc(rho, kb), ident_b[:]),
                             reads=[vkey, CK], writes=[("pss", 3 - half)])
                    ng = len(grp)
                    src = pbank[:, 0:ng * 128].rearrange("p (b f) -> p b f", b=ng)
                    KEV = os.environ.get("KEV", "dD")
                    if "a" in KEV:
                        P.op("act", actf(Vp0[:, g0:g0 + ng, 0:64], src[:, :, 0:64], AF.Copy), reads=[("pss", 3 - half)], writes=["Vp0"])
                    if "d" in KEV:
                        P.op("dve", cp(Vp1[:, g0:g0 + ng, 64:128], src[:, :, 64:128]), reads=[("pss", 3 - half)], writes=["Vp1"])
                    if "D" in KEV:
                        P.op("dve", cp(Vp0[:, g0:g0 + ng, 0:64], src[:, :, 0:64]), reads=[("pss", 3 - half)], writes=["Vp0"])
                if KP3 <= 2:
                    continue
                if dil == 1:
                    groups = [[(0, q) for q in range(g * 4, g * 4 + 4)] for g in range(4)]
                elif dil == 4:
                    groups = [[(rho, q) for q in range(4)] for rho in range(4)]
                else:
                    groups = [[(rho, 0) for rho in range(g * 4, g * 4 + 4)] for g in range(4)]
                qloc = {}
                for gi_, grp in enumerate(groups):
                    for i, rq in enumerate(grp):
                        qloc[rq] = (gi_, i)
                gbank = {}
                def pv_part(rho, kb, qbs, lo, blk, pts):
                    for q in qbs:
                        gi_, i = qloc[(rho, q)]
                        if gi_ not in gbank:
                            gbank[gi_] = gcnt[0] % 2
                            gcnt[0] += 1
                        gb = gbank[gi_]
                        xo = (q * 128 - lo)
                        is_prev = (kb == q)
                        for hh in range(2):
                            st = is_prev and hh == 0
                            sp_ = (not is_prev) and hh == 1
                            Vp = Vp0 if hh == 0 else Vp1
                            on = on0_b if hh == 0 else on1_b
                            P.op("pe", mm(ps[2 + gb][:, i * 128:(i + 1) * 128], Vp[:, blk, :], pt[pts[hh]][:, xo:xo + 128], start=st, stop=sp_),
                                 reads=["Vp0" if hh == 0 else "Vp1", ("pt", pts[hh])], writes=[("ps", 2 + gb)])
                            P.op("pe", mm(ps[4 + gb][:, i * 128:(i + 1) * 128], on[:], pt[pts[hh]][:, xo:xo + 128], start=st, stop=sp_),
                                 reads=[CK, ("pt", pts[hh])], writes=[("ps", 4 + gb)])
                        if (not is_prev) and i == len(groups[gi_]) - 1 and KP3 <= 4:
                            del gbank[gi_]
                        elif (not is_prev) and i == len(groups[gi_]) - 1:
                            if dil == 1:
                                oa = lambda a, g=gi_: a[:, g * 512:(g + 1) * 512]
                                pin = lambda b_: ps[b_][:, :]
                            elif dil == 4:
                                oa = lambda a, r=gi_: a[:].rearrange("p (m r) -> p m r", r=4)[:, :, r]
                                pin = lambda b_: ps[b_][:, :]
                            else:
                                oa = lambda a, g=gi_: a[:].rearrange("p (m r) -> p m r", r=16)[:, :, g * 4:(g + 1) * 4]
                                pin = lambda b_: ps[b_][:, :].rearrange("p (r m) -> p m r", r=4)
                            if dil == 1:
                                P.op("act", actf(oa(accn), pin(2 + gb), AF.Copy), reads=[("ps", 2 + gb)], writes=["accn"])
                                P.op("dve", cp(oa(accd), pin(4 + gb)), reads=[("ps", 4 + gb)], writes=["accd"])
                            else:
                                P.op("dve", tt(oa(accn), oa(accn), pin(2 + gb), ALU.add), reads=[("ps", 2 + gb), "accn"], writes=["accn"])
                                P.op("dve", tt(oa(accd), oa(accd), pin(4 + gb), ALU.add), reads=[("ps", 4 + gb), "accd"], writes=["accd"])
                            del gbank[gi_]
                pend = []
                for rho in range(nres):
                    for kb in range(nkb):
                        qbs = [q for q in (kb - 1, kb) if 0 <= q < nqb]
                        lo, hi = qbs[0] * 128, (qbs[-1] + 1) * 128
                        x0 = 0 if (kb - 1) >= 0 else 128
                        N = hi - lo
                        pts = []
                        for hh in range(2):
                            pbse = hh * 64
                            si = scnt[0] % 4
                            scnt[0] += 1
                            spsum = ps[(0, 1, 6, 7)[si]][:, 0:N]
                            P.op("pe", mm(spsum, karr(rho, kb)[pbse:pbse + 64, :], qarr(rho, lo, hi)[pbse:pbse + 64, :]),
                                 reads=[kkey, qkey], writes=[("pss", si)])
                            ei = scnt[0] % 6
                            bias = HB if kb == 0 else ZB
                            P.op("act", actf(et[ei][:, 0:N], spsum, AF.Exp, bias=bias), reads=[("pss", si), CK], writes=[("et", ei)])
                            P.op("dve", tt(pt[ei][:, 0:N], et[ei][:, 0:N], EB[:, di * 8 + 2 * c + hh, x0:x0 + N], ALU.mult),
                                 reads=[("et", ei), CK], writes=[("pt", ei)])
                            pts.append(ei)
                        blk = rho * nkb + kb
                        if KP3 <= 3:
                            continue
                        pend.append((rho, kb, qbs, lo, blk, pts))
                        if len(pend) > 1:
                            pv_part(*pend.pop(0))
                while pend:
                    pv_part(*pend.pop(0))
            P.op("dve", lambda e: e.reciprocal(out=accd[:], in_=accd[:]), reads=["accd"], writes=["accd"])
            P.op("dve", tt(ao[:], accn[:], accd[:], ALU.mult), reads=["accn", "accd"], writes=["ao"])
            P.dma("sp", dm(MIX[c * 128:(c + 1) * 128, 0:NTOK], ao[:]), reads=["ao"], writes=[("MIXp", c)])
        FENCE()
        if l == 0 and KSTOP < 5:
            return

        sc[0] = S0
        gl = [SA(f"gl{i}", [128, 30 + NTOK], BF16) for i in range(2)]
        yacc = [SA(f"yacc{i}", [128, NCOL], F32) for i in range(2)]
        sqf = [SA(f"sqf{i}", [128, 512], F32) for i in range(2)]
        rsc = SA("rsc", [128, 512], F32)
        zb = SA("zb", [128, 2, 512], BF16)
        cob = [SA(f"cob{i}", [128, 512], BF16) for i in range(2)]
        pwb2 = SA("pwb2", [128, 2, 256], BF16)
        pbd2 = [SA(f"pbd2{i}", [128, 128], BF16) for i in range(2)]
        ce = [SA(f"ce{i}", [128, 16 + NTOK], F32) for i in range(2)]
        s2 = SA("s2", [128, 16 + NTOK], F32)
        s4 = SA("s4", [128, 16 + NTOK], F32)
        dpl = SA("dpl", [128, NCOL], BF16)
        ctm = SA("ctm", [128, NS, 31], F32)
        cfx = SA("cfx", [128, 16], F32)
        assert sc[0] <= 229376, sc[0]
        P.dma("pool", dm(pwb2[:], convpw_in[l].rearrange("(kc p) f -> p kc f", p=128)), writes=["pwb2"])
        for cc in range(2):
            P.op("pool", lambda e, cc=cc: e.memset(pbd2[cc][:], 0.0), writes=[("pbd2", cc)])
            for gg in range(2):
                P.dma("pool", dm(pbd2[cc][gg * 64:(gg + 1) * 64, gg * 64:(gg + 1) * 64], poolw_in[l, cc * 2 + gg]), writes=[("pbd2", cc)])
        for cc in range(2):
            P.dma("sp", dm(gl[cc][:, 0:30], UGr(8 + cc)[:, NTOK - 30:NTOK]), writes=[("gl", cc)])
            P.dma("sp", dm(gl[cc][:, 30:], UXr(8 + cc)), writes=[("gl", cc)])
            P.op("dve", tsc(gl[cc][:, 0:30], gl[cc][:, 0:30], HM, None, ALU.mult), reads=[("gl", cc), CK], writes=[("gl", cc)])
            eng = "dve"
            P.op(eng, tsc(yacc[cc][:, 0:NTOK], gl[cc][:, 0:NTOK], CW(l, cc, 0), VEC(l, 0, cc), ALU.mult, ALU.add),
                 reads=[("gl", cc), CK], writes=[("yacc", cc)])
            for w in range(1, 31):
                P.op(eng, stt(yacc[cc][:, 0:NTOK], gl[cc][:, w:w + NTOK], CW(l, cc, w), yacc[cc][:, 0:NTOK], ALU.mult, ALU.add),
                     reads=[("gl", cc), CK, ("yacc", cc)], writes=[("yacc", cc)])
            cwv = pp[:, 52 + (l * 2 + cc) * 31:52 + (l * 2 + cc) * 31 + 31]
            for b in range(NS):
                P.op("dve", tt(ctm[:, b, :], csT[:, cc, b, :], cwv, ALU.mult), reads=["csT", CK], writes=["ctm"])
            P.op("dve", lambda e, cc=cc: e.tensor_reduce(out=yacc[cc][:, NTOK:NCOL], in_=ctm[:], axis=AX.X, op=ALU.add),
                 reads=["ctm"], writes=[("yaccs", cc)])
            P.op("dve", tsc(yacc[cc][:, NTOK:NCOL], yacc[cc][:, NTOK:NCOL], VEC(l, 0, cc), None, ALU.add), reads=[("yaccs", cc), CK], writes=[("yaccs", cc)])
        for ti in range(5):
            c0, n = TILES[ti]
            yk = (lambda cc: ("yacc", cc)) if ti < 4 else (lambda cc: ("yaccs", cc))
            for cc in range(2):
                P.op("pe", mm(ps[0][:, 0:n], ones_f[:], yacc[cc][:, c0:c0 + n], start=(cc == 0), stop=(cc == 1)),
                     reads=[yk(cc), CK], writes=[("ps", 0)])
            for cc in range(2):
                P.op("dve", stt(yacc[cc][:, c0:c0 + n], ps[0][:, 0:n], -1.0 / 256, yacc[cc][:, c0:c0 + n], ALU.mult, ALU.add),
                     reads=[("ps", 0), yk(cc)], writes=[yk(cc)])
                P.op("act", actf(sqf[cc][:, 0:n], yacc[cc][:, c0:c0 + n], AF.Square), reads=[yk(cc)], writes=[("sqf", cc)])
            for cc in range(2):
                P.op("pe", mm(ps[1][:, 0:n], ones_f[:], sqf[cc][:, 0:n], start=(cc == 0), stop=(cc == 1)),
                     reads=[("sqf", cc), CK], writes=[("ps", 1)])
            P.op("act", actf(rsc[:, 0:n], ps[1][:, 0:n], AF.Sqrt, bias=EPSC, scale=1.0 / 256), reads=[("ps", 1), CK], writes=["rsc"])
            P.op("dve", rcp(rsc[:, 0:n], rsc[:, 0:n]), reads=["rsc"], writes=["rsc"])
            for cc in range(2):
                P.op("dve", tt(sqf[cc][:, 0:n], yacc[cc][:, c0:c0 + n], rsc[:, 0:n], ALU.mult), reads=[yk(cc), "rsc"], writes=[("sqf", cc)])
                P.op("dve", tsc(sqf[cc][:, 0:n], sqf[cc][:, 0:n], VEC(l, 1, cc), VEC(l, 2, cc), ALU.mult, ALU.add),
                     reads=[("sqf", cc), CK], writes=[("sqf", cc)])
                P.op("act", actf(zb[:, cc, 0:n], sqf[cc][:, 0:n], AF.Silu), reads=[("sqf", cc)], writes=[("zb", cc)])
            for co in range(2):
                for ci in range(2):
                    P.op("pe", mm(ps[2 + co][:, 0:n], pwb2[:, ci, co * 128:(co + 1) * 128], zb[:, ci, 0:n], start=(ci == 0), stop=(ci == 1)),
                         reads=["pwb2", ("zb", ci)], writes=[("ps", 2 + co)])
                P.op("act", actf(cob[co][:, 0:n], ps[2 + co][:, 0:n], AF.Copy), reads=[("ps", 2 + co)], writes=[("cob", co)])
                P.dma("sp", dm(MIX[(4 + co) * 128:(5 + co) * 128, c0:c0 + n], cob[co][:, 0:n]), reads=[("cob", co)], writes=[("MIXc", co, ti)])
        W_ = 16 + NTOK
        for cc in range(2):
            x = ce[cc]
            P.dma("sp", dm(dpl[:, 0:15], UGr(10 + cc)[:, NTOK - 15:NTOK]), writes=["dpl"])
            P.op("dve", lambda e, cc=cc: e.memset(ce[cc][:, 0:1], 0.0), writes=[("ce", cc)])
            P.op("dve", tsc(x[:, 1:16], dpl[:, 0:15], HM, None, ALU.mult), reads=["dpl", CK], writes=[("ce", cc)])
            P.dma("sp", dm(dpl[:, 0:NTOK], UXr(10 + cc)), reads=[("ce", cc)], writes=["dpl"])
            P.op("dve", cp(x[:, 16:], dpl[:, 0:NTOK]), reads=["dpl"], writes=[("ce", cc)])
            P.op("dve", tt(s2[:, 1:W_], x[:, 1:W_], x[:, 0:W_ - 1], ALU.add), reads=[("ce", cc)], writes=["s2"])
            P.op("dve", tt(s4[:, 3:W_], s2[:, 3:W_], s2[:, 1:W_ - 2], ALU.add), reads=["s2"], writes=["s4"])
            if cc == 1:
                P.op("dve", tt(s2[:, 7:W_], s4[:, 7:W_], s4[:, 3:W_ - 4], ALU.add), reads=["s4", "s2"], writes=["s2"])
                P.op("dve", tt(s4[:, 15:W_], s2[:, 15:W_], s2[:, 7:W_ - 8], ALU.add), reads=["s2", "s4"], writes=["s4"])
            for half, srcs in ((0, s2), (1, s4)):
                pr = slice(half * 64, (half + 1) * 64)
                w = (2, 4, 8, 16)[cc * 2 + half]
                P.op("dve", stt(dpl[pr, 0:NTOK], srcs[pr, 16:], pp[pr, 194 + cc:195 + cc], x[pr, 16:], ALU.mult, ALU.subtract),
                     reads=["s2", "s4", ("ce", cc), CK, "dpl"], writes=["dpl"])
                P.op("dve", tt(cfx[pr, :], srcs[pr, 16:32], ICNT(cc)[pr, :], ALU.mult), reads=["s2", "s4", CK], writes=["cfx"])
                P.op("dve", tt(dpl[pr, 0:16], cfx[pr, :], x[pr, 16:32], ALU.subtract), reads=["cfx", ("ce", cc), "dpl"], writes=["dpl"])
                P.op("dve", lambda e, pr=pr, cc=cc, w=w: e.tensor_reduce(out=cfx[pr, :], in_=cpT[pr, cc, :, 16 - w:16], axis=AX.X, op=ALU.add),
                     reads=["cpT", "cfx", "dpl"], writes=["cfx"])
                P.op("dve", stt(dpl[pr, NTOK:NCOL], cfx[pr, :], 1.0 / w, cpT[pr, cc, :, 15], ALU.mult, ALU.subtract),
                     reads=["cfx", "cpT", "dpl"], writes=["dpl"])
            for ti in range(5):
                c0, n = TILES[ti]
                P.op("pe", mm(ps[4][:, 0:n], pbd2[cc][:], dpl[:, c0:c0 + n]), reads=[("pbd2", cc), "dpl"], writes=[("ps", 4)])
                P.op("dve", tsc(cob[ti % 2][:, 0:n], ps[4][:, 0:n], VEC(l, 3, cc), None, ALU.mult), reads=[("ps", 4), CK], writes=[("cob", ti % 2)])
                P.dma("sp", dm(MIX[(6 + cc) * 128:(7 + cc) * 128, c0:c0 + n], cob[ti % 2][:, 0:n]), reads=[("cob", ti % 2)], writes=[("MIXq", cc, ti)])
        FENCE()
        if l == 0 and KSTOP < 6:
            return

        sc[0] = S0
        mixT = SA("mixT", [128, 8, NCOL], BF16)
        wo = [SA(f"wo{i}", [128, 8, 128], BF16) for i in range(2)]
        for c in range(8):
            P.dma("sp", dm(mixT[:, c, :], MIX[c * 128:(c + 1) * 128, :]), writes=[("mixT", c)])
        Wo = wout_in[l].rearrange("(kc p) f -> p kc f", p=128)
        cnt2 = 0
        for dc in range(8):
            b = dc % 2
            P.dma("pool", dm(wo[b][:], Wo[:, :, dc * 128:(dc + 1) * 128]), writes=[("wo", b)])
            for ti in range(5):
                c0, n = TILES[ti]
                bk = cnt2 % 2
                cnt2 += 1
                for kc in range(8):
                    P.op("pe", mm(ps[bk][:, 0:n], wo[b][:, kc, :], mixT[:, kc, c0:c0 + n], start=(kc == 0), stop=(kc == 7)),
                         reads=[("wo", b), ("mixT", kc)], writes=[("ps", bk)])
                P.op("dve", tt(hT[:, dc, c0:c0 + n], ps[bk][:, 0:n], hT[:, dc, c0:c0 + n], ALU.add),
                     reads=[("ps", bk), ("hT", dc, ti)], writes=[("hT", dc, ti)])

    FENCE()
    for l in range(2):
        if l == 0 and KSTOP < 1:
            break
        ffn(l, 0)
        FENCE()
        if l == 0 and KSTOP < 2:
            break
        mixer(l)
        FENCE()
        if l == 0 and KSTOP < 7:
            break
        ffn(l, 1)
        FENCE()
        if l == 0 and KSTOP < 8:
            break

    sc[0] = S0
    yt = [SA(f"yt{i}", [128, 1024], F32) for i in range(2)]
    for blk in range(17):
        n = 128 if blk < 16 else NS
        yb = yt[blk % 2]
        ti = min(blk // 4, 4)
        for g in range(2):
            for dd in range(4):
                dc = g * 4 + dd
                P.op("pe", tr(ps[g][0:n, dd * 128:(dd + 1) * 128], hT[:, dc, blk * 128:blk * 128 + n], ident_f[:]),
                     reads=[("hT", dc, ti), CK], writes=[("ps", g)])
            if g == 0:
                P.op("act", actf(yb[0:n, 0:512], ps[0][0:n, :], AF.Copy), reads=[("ps", 0)], writes=[("yt", blk % 2)])
            else:
                P.op("dve", cp(yb[0:n, 512:1024], ps[1][0:n, :]), reads=[("ps", 1)], writes=[("yt", blk % 2)])
        dst = y_out[blk * 128:(blk + 1) * 128, :] if blk < 16 else ys_out
        P.dma("sp", dm(dst, yb[0:n, :]), reads=[("yt", blk % 2)], writes=[])

    emit(nc, P, cc_sems)
    return nc


def emit(nc, P, cc_sems):
    engs = Prog.ENG
    for e in engs:
        for r in P.recs[e]:
            for d in r.deps:
                if d.kind == "c" and not (d.eng == "pe" and r.eng == "pe" and r.kind == "c"):
                    d.inc = True
    cnt = {e: 0 for e in engs}
    for e in engs:
        for r in P.recs[e]:
            if r.kind == "c" and r.inc:
                cnt[e] += 1
                r.val = cnt[e]
    S = {e: nc.alloc_semaphore(f"S_{e}") for e in engs}
    NQ = {"sp": 12, "pool": 6, "act": 2, "pe": 1, "dve": 1}
    dsem = {e: [nc.alloc_semaphore(f"D_{e}{i}") for i in range(NQ[e])] for e in engs}
    dcount = {e: [0] * NQ[e] for e in engs}
    dlast = {e: [None] * NQ[e] for e in engs}
    dk = {e: 0 for e in engs}
    for e in engs:
        for r in P.recs[e]:
            if r.kind == "d":
                i = dk[e] % NQ[e]
                dk[e] += 1
                r.sem = dsem[e][i]
                dcount[e][i] += 16
                r.dval = dcount[e][i]
                if dlast[e][i] is not None:
                    r.deps.append(dlast[e][i])
                dlast[e][i] = r
            elif r.kind == "cc":
                r.dval = 1

    def run(engname, eng):
        waited = {}
        for r in P.recs[engname]:
            for d in r.deps:
                if d.kind == "c":
                    if d.eng == "pe" and engname == "pe" and r.kind == "c":
                        continue
                    key, sem, val = ("S", d.eng), S[d.eng], d.val
                else:
                    key, sem, val = ("D", id(d.sem)), d.sem, d.dval
                if waited.get(key, 0) >= val:
                    continue
                eng.wait_ge(sem, val)
                waited[key] = val
            ins = r.fn(eng)
            if r.kind == "d":
                ins.then_inc(r.sem, 16)
            elif r.kind == "cc":
                ins.then_inc(r.sem)
            elif r.inc:
                ins.then_inc(S[engname], 1)
        if engname == "sp":
            for e in engs:
                for i in range(NQ[e]):
                    if dcount[e][i] > 0:
                        eng.wait_ge(dsem[e][i], dcount[e][i])

    with nc.Block() as block:
        @block.tensor
        def _(e):
            run("pe", e)

        @block.scalar
        def _(e):
            run("act", e)

        @block.vector
        def _(e):
            run("dve", e)

        @block.gpsimd
        def _(e):
            run("pool", e)

        @block.sync
        def _(e):
            run("sp", e)


_NC = None


def _consts():
    cst = np.zeros((128, 1024), np.float32)
    cst[:, 0:128] = np.eye(128, dtype=np.float32)
    cst[:, 128:256] = 1.0
    bd = np.zeros((128, 128), np.float32)
    bd[:64, :64] = 1.0
    bd[64:, 64:] = 1.0
    cst[:, 256:384] = bd
    cst[:, 384:448] = 1.0
    cst[:, 512 + 64:640] = 1.0
    oh = np.zeros((33, 1152), np.float32)
    for d, dil in enumerate(DILS):
        for z in range(384):
            dl = z - 127
            if 0 <= dl <= 128:
                oh[int(_bucket_np(dil * dl)), d * 384 + z] = 1.0
            else:
                oh[32, d * 384 + z] = 1.0
    ohs = np.zeros((32, 400), np.float32)
    for d, dil in enumerate(DILS):
        for i in range(128):
            ohs[int(_bucket_np(dil * (128 - i))), d * 128 + i] = 1.0
    ohs[0, 384:400] = 1.0
    bmask = np.zeros((8, 512), np.float32)
    for h in range(8):
        bmask[h, h * 64:(h + 1) * 64] = 1.0
    esel = np.zeros((8, 256), np.float32)
    for b in range(16):
        esel[:, b * 16 + b] = 1.0
    return cst, oh, ohs, bmask, esel


def kernel(x_prompt, x_sample, cache_attn_k, cache_attn_v, cache_conv, cache_pool, rel_bias,
           g_ffn1, w_ffn1_gu, w_ffn1_down, g_mix, w_in, g_q, g_k, conv_w, conv_b, conv_ln_g,
           conv_ln_b, conv_pw, pool_w, pool_scale, w_out, g_ffn2, w_ffn2_gu, w_ffn2_down, _prep_only=False):
    global _NC
    f = lambda a: np.ascontiguousarray(np.asarray(a, dtype=np.float32))
    x_prompt, x_sample = f(x_prompt), f(x_sample)
    ck, cv = np.asarray(cache_attn_k), np.asarray(cache_attn_v)
    cst, oh, ohs, bmask, esel = _consts()
    pp = np.zeros((128, 232), np.float32)
    gs = [f(g_ffn1), f(g_mix), f(g_ffn2)]
    for l in range(2):
        for k in range(3):
            pp[:, (l * 3 + k) * 8:(l * 3 + k) * 8 + 8] = gs[k][l].reshape(8, 128).T
        pp[:, 48 + l] = np.tile(f(g_q)[l], 2)
        pp[:, 50 + l] = np.tile(f(g_k)[l], 2)
        for cc in range(2):
            pp[:, 52 + (l * 2 + cc) * 31:52 + (l * 2 + cc) * 31 + 31] = f(conv_w)[l][:, cc * 128:(cc + 1) * 128].T
            for k, v in enumerate((conv_b, conv_ln_g, conv_ln_b, pool_scale)):
                pp[:, 176 + (l * 4 + k) * 2 + cc] = f(v)[l][cc * 128:(cc + 1) * 128]
    vecs = np.stack([np.stack([f(conv_b)[l], f(conv_ln_g)[l], f(conv_ln_b)[l], f(pool_scale)[l]]) for l in range(2)])
    wins = np.array([2, 4, 8, 16])
    in_maps = []
    for i in range(8):
        b, half = i // 2, i % 2
        ppi = pp.copy()
        ppi[:, 231] = EPS
        ppi[:, 192] = 1.0 if half else 0.0
        ppi[:, 193] = 0.0 if half else NEG
        for cc in range(2):
            w = np.repeat(wins[cc * 2:cc * 2 + 2], 64).astype(np.float32)
            ppi[:, 194 + cc] = 1.0 / w
            for t in range(16):
                ppi[:, 196 + cc * 16 + t] = 1.0 / np.minimum(w, half * NTOK + t + 1)
        sl = slice(i * NS, (i + 1) * NS)
        in_maps.append({
            "x": np.ascontiguousarray(x_prompt[b, half * NTOK:(half + 1) * NTOK]),
            "xs": np.ascontiguousarray(x_sample[sl, 0]),
            "ck": np.ascontiguousarray(ck[:, sl].reshape(2, NS, -1, 512), dtype=np.float32),
            "cv": np.ascontiguousarray(cv[:, sl].reshape(2, NS, -1, 512), dtype=np.float32),
            "cconv": np.ascontiguousarray(f(cache_conv)[:, sl]),
            "cpool": np.ascontiguousarray(f(cache_pool)[:, sl]),
            "relb": f(rel_bias),
            "wgu1": f(w_ffn1_gu), "wgu2": f(w_ffn2_gu), "wdn1": f(w_ffn1_down), "wdn2": f(w_ffn2_down),
            "win": f(w_in), "wout": f(w_out), "convpw": f(conv_pw), "poolw": f(pool_w),
            "pp": ppi, "cst": cst, "oh": oh, "ohs": ohs, "bmask": bmask, "esel": esel,
            "convw": f(conv_w), "vecs": vecs,
        })
    if _prep_only:
        return in_maps
    if _NC is None:
        _NC = build_program()
    res = run_bass_kernel_spmd(_NC, in_maps, core_ids=list(range(8)))
    return assemble(res.results)


def assemble(R, ncores=8):
    y_prompt = np.zeros((4, 4096, 1024), np.float32)
    y_sample = np.zeros((128, 1, 1024), np.float32)
    nkp = np.zeros((2, 4, 2048, 8, 64), np.float32)
    nvp = np.zeros((2, 4, 2048, 8, 64), np.float32)
    ncp = np.zeros((2, 4, 30, 256), np.float32)
    npp = np.zeros((2, 4, 15, 256), np.float32)
    nks = np.zeros((2, 128, 1, 8, 64), np.float32)
    nvs = np.zeros((2, 128, 1, 8, 64), np.float32)
    ncs = np.zeros((2, 128, 30, 256), np.float32)
    nps = np.zeros((2, 128, 15, 256), np.float32)
    for i in range(ncores):
        b, half = i // 2, i % 2
        r = R[i]
        y_prompt[b, half * NTOK:(half + 1) * NTOK] = r["y"]
        sl = slice(i * NS, (i + 1) * NS)
        y_sample[sl, 0] = r["ys"]
        if half == 1:
            nkp[:, b] = r["kp"].reshape(2, 2048, 8, 64)
            nvp[:, b] = r["vp"].reshape(2, 2048, 8, 64)
            ncp[:, b] = r["cvp"]
            npp[:, b] = r["plp"]
        nks[:, sl, 0] = r["kso"].reshape(2, NS, 8, 64)
        nvs[:, sl, 0] = r["vso"].reshape(2, NS, 8, 64)
        ncs[:, sl] = r["cvs"]
        nps[:, sl] = r["pls"]
    return (y_prompt, y_sample, nkp, nvp, ncp, npp, nks, nvs, ncs, nps)
```
